# Optimizing a Trainium2 kernel written in Bass

```python
import jax, jax.numpy as jnp
from jax import lax
import numpy as np

D_MODEL = 1024
BATCH = 4
SEQ = 4096
DEPTH = 1
DEC_BATCH = 16
DEC_SEQ = 32
PAST_LEN = 1024

CHUNK = 64
N_HEADS = 16
N_KV_HEADS = 2
HEAD_DIM = 64
Q_PER_KV = N_HEADS // N_KV_HEADS
WINDOW = 128
WINDOW_CHUNKS = WINDOW // CHUNK
D_POOL = D_MODEL // 2
N_POOL_GROUPS = 4
POOL_GROUP = D_POOL // N_POOL_GROUPS
POOL_WINDOWS = (2, 4, 8, 16)
POOL_HIST = max(POOL_WINDOWS) - 1
D_Q = N_HEADS * HEAD_DIM
D_KV = N_KV_HEADS * HEAD_DIM
D_IN = D_POOL + D_Q + 2 * D_KV + 2 * D_MODEL
SPLITS = (D_POOL, D_POOL + D_Q, D_POOL + D_Q + D_KV, D_POOL + D_Q + 2 * D_KV, D_POOL + D_Q + 2 * D_KV + D_MODEL)
D_FF = 2816
EPS = 1e-6

kernel_name = "streaming_pool_swa_hybrid_step"


def rmsnorm(x, g):
    xf = x.astype(jnp.float32)
    y = xf * lax.rsqrt(jnp.mean(xf * xf, axis=-1, keepdims=True) + EPS)
    return (y * g.astype(jnp.float32)).astype(x.dtype)


def half_swiglu(x, g, w_in, w_out):
    h = rmsnorm(x, g)
    gate, up = jnp.split(h @ w_in, 2, axis=-1)
    return x + 0.5 * ((jax.nn.silu(gate) * up) @ w_out)


def multiscale_pool(u, hist, w_group, b_group, scale):
    B, S, _ = u.shape
    uf = u.astype(jnp.float32)
    if hist is None:
        ext = jnp.pad(uf, ((0, 0), (POOL_HIST, 0), (0, 0)))
    else:
        ext = jnp.concatenate([hist.astype(jnp.float32), uf], axis=1)
    cs = jnp.pad(jnp.cumsum(ext, axis=1), ((0, 0), (1, 0), (0, 0)))
    end = cs[:, POOL_HIST + 1:]
    outs = []
    for gi, w in enumerate(POOL_WINDOWS):
        sl = slice(gi * POOL_GROUP, (gi + 1) * POOL_GROUP)
        s = end[..., sl] - cs[:, POOL_HIST + 1 - w:POOL_HIST + 1 - w + S, sl]
        if hist is None:
            cnt = jnp.minimum(jnp.arange(1, S + 1), w).astype(jnp.float32)[None, :, None]
        else:
            cnt = jnp.float32(w)
        outs.append(s / cnt - uf[..., sl])
    pooled = jnp.concatenate(outs, axis=-1).reshape(B, S, N_POOL_GROUPS, POOL_GROUP)
    y = jnp.einsum('bsgc,gcd->bsgd', pooled, w_group.astype(jnp.float32)) + b_group.astype(jnp.float32)
    y = y.reshape(B, S, D_POOL) * scale.astype(jnp.float32)
    return y.astype(u.dtype)


def attend_with_sinks(qg, k, v, sinks, mask):
    s = jnp.einsum('...qkgd,...pkd->...kgqp', qg.astype(jnp.float32), k.astype(jnp.float32)) * (HEAD_DIM ** -0.5)
    if mask is not None:
        s = jnp.where(mask, s, jnp.finfo(jnp.float32).min)
    sink = sinks.astype(jnp.float32).reshape(N_KV_HEADS, Q_PER_KV)[:, :, None, None]
    m = jnp.maximum(jnp.max(s, axis=-1, keepdims=True), sink)
    p = jnp.exp(s - m)
    w = p / (jnp.sum(p, axis=-1, keepdims=True) + jnp.exp(sink - m))
    o = jnp.einsum('...kgqp,...pkd->...qkgd', w, v.astype(jnp.float32))
    return o.astype(qg.dtype)


def swa_prompt(q, k, v, sinks):
    B, S = q.shape[:2]
    nb = S // CHUNK
    qb = q.reshape(B, nb, CHUNK, N_KV_HEADS, Q_PER_KV, HEAD_DIM)
    pad = ((0, 0), (WINDOW, 0), (0, 0), (0, 0))
    kp = jnp.pad(k, pad).reshape(B, nb + WINDOW_CHUNKS, CHUNK, N_KV_HEADS, HEAD_DIM)
    vp = jnp.pad(v, pad).reshape(B, nb + WINDOW_CHUNKS, CHUNK, N_KV_HEADS, HEAD_DIM)
    kb = jnp.concatenate([kp[:, j:j + nb] for j in range(WINDOW_CHUNKS + 1)], axis=2)
    vb = jnp.concatenate([vp[:, j:j + nb] for j in range(WINDOW_CHUNKS + 1)], axis=2)
    key_chunk = jnp.arange(nb)[:, None] + jnp.arange((WINDOW_CHUNKS + 1) * CHUNK)[None, :] // CHUNK - WINDOW_CHUNKS
    mask = (key_chunk >= 0)[None, :, None, None, None, :]
    o = attend_with_sinks(qb, kb, vb, sinks, mask)
    return o.reshape(B, S, D_Q)


def swa_sample(q, k, v, k_hist, v_hist, sinks):
    B, T = q.shape[:2]
    kf = jnp.concatenate([k_hist.astype(k.dtype), k], axis=1)
    vf = jnp.concatenate([v_hist.astype(v.dtype), v], axis=1)
    o = attend_with_sinks(q.reshape(B, T, N_KV_HEADS, Q_PER_KV, HEAD_DIM), kf, vf, sinks, None)
    return o.reshape(B, T, D_Q), kf[:, -WINDOW:], vf[:, -WINDOW:]


def layer(x, hist, p):
    (norm_ffn1, ffn1_w_in, ffn1_w_out, norm_mix, w_in, b_gate, pool_w, pool_b, pool_scale,
     q_norm, k_norm, sinks, w_pool_proj, w_attn_proj, w_out, norm_ffn2, ffn2_w_in, ffn2_w_out) = p
    B, S, _ = x.shape
    x = half_swiglu(x, norm_ffn1, ffn1_w_in, ffn1_w_out)
    h = rmsnorm(x, norm_mix)
    u, q, k, v, ga, gb = jnp.split(h @ w_in, SPLITS, axis=-1)
    q = rmsnorm(q.reshape(B, S, N_HEADS, HEAD_DIM), q_norm)
    k = rmsnorm(k.reshape(B, S, N_KV_HEADS, HEAD_DIM), k_norm)
    v = v.reshape(B, S, N_KV_HEADS, HEAD_DIM)
    if hist is None:
        ya = multiscale_pool(u, None, pool_w, pool_b, pool_scale)
        yb = swa_prompt(q, k, v, sinks)
        new_pool, new_k, new_v = u[:, -POOL_HIST:], k[:, -WINDOW:], v[:, -WINDOW:]
    else:
        pool_hist, k_hist, v_hist = hist
        ya = multiscale_pool(u, pool_hist, pool_w, pool_b, pool_scale)
        yb, new_k, new_v = swa_sample(q, k, v, k_hist, v_hist, sinks)
        new_pool = jnp.concatenate([pool_hist.astype(u.dtype), u], axis=1)[:, -POOL_HIST:]
    g_a = jax.nn.sigmoid(ga + b_gate[0])
    g_b = jax.nn.sigmoid(gb + b_gate[1])
    x = x + (g_a * (ya @ w_pool_proj) + g_b * (yb @ w_attn_proj)) @ w_out
    x = half_swiglu(x, norm_ffn2, ffn2_w_in, ffn2_w_out)
    return x, new_pool, new_k, new_v


def setup_inputs(seed: int = 0) -> dict:
    key = jax.random.key(seed)
    ks = jax.random.split(key, 32)
    nrm = lambda i, shape, s: jax.random.normal(ks[i], shape, jnp.float32) * s
    gain = lambda i, shape: 1.0 + 0.05 * jax.random.normal(ks[i], shape, jnp.float32)
    L = DEPTH
    return {
        "x_prompt": nrm(0, (BATCH, SEQ, D_MODEL), 1.0),
        "x_sample": nrm(1, (DEC_BATCH, DEC_SEQ, D_MODEL), 1.0),
        "state_pool": nrm(2, (L, DEC_BATCH, POOL_HIST, D_POOL), 1.0),
        "cache_k": nrm(3, (L, DEC_BATCH, WINDOW, N_KV_HEADS, HEAD_DIM), 1.0),
        "cache_v": nrm(4, (L, DEC_BATCH, WINDOW, N_KV_HEADS, HEAD_DIM), 1.0),
        "norm_ffn1": gain(5, (L, D_MODEL)),
        "ffn1_w_in": nrm(6, (L, D_MODEL, 2 * D_FF), D_MODEL ** -0.5),
        "ffn1_w_out": nrm(7, (L, D_FF, D_MODEL), D_FF ** -0.5),
        "norm_mix": gain(8, (L, D_MODEL)),
        "w_in": nrm(9, (L, D_MODEL, D_IN), D_MODEL ** -0.5),
        "b_gate": nrm(10, (L, 2, D_MODEL), 0.1),
        "pool_w": nrm(11, (L, N_POOL_GROUPS, POOL_GROUP, POOL_GROUP), POOL_GROUP ** -0.5),
        "pool_b": nrm(12, (L, N_POOL_GROUPS, POOL_GROUP), 0.02),
        "pool_scale": gain(13, (L, D_POOL)),
        "q_norm": gain(14, (L, HEAD_DIM)),
        "k_norm": gain(15, (L, HEAD_DIM)),
        "sinks": nrm(16, (L, N_HEADS), 0.5),
        "w_pool_proj": nrm(17, (L, D_POOL, D_MODEL), D_POOL ** -0.5),
        "w_attn_proj": nrm(18, (L, D_Q, D_MODEL), D_Q ** -0.5),
        "w_out": nrm(19, (L, D_MODEL, D_MODEL), D_MODEL ** -0.5),
        "norm_ffn2": gain(20, (L, D_MODEL)),
        "ffn2_w_in": nrm(21, (L, D_MODEL, 2 * D_FF), D_MODEL ** -0.5),
        "ffn2_w_out": nrm(22, (L, D_FF, D_MODEL), D_FF ** -0.5),
    }


def reference(x_prompt, x_sample, state_pool, cache_k, cache_v, norm_ffn1, ffn1_w_in, ffn1_w_out,
              norm_mix, w_in, b_gate, pool_w, pool_b, pool_scale, q_norm, k_norm, sinks,
              w_pool_proj, w_attn_proj, w_out, norm_ffn2, ffn2_w_in, ffn2_w_out):
    xp, xs = x_prompt, x_sample
    pp, kp, vp, ps, ksm, vsm = [], [], [], [], [], []
    for l in range(DEPTH):
        p = (norm_ffn1[l], ffn1_w_in[l], ffn1_w_out[l], norm_mix[l], w_in[l], b_gate[l], pool_w[l],
             pool_b[l], pool_scale[l], q_norm[l], k_norm[l], sinks[l], w_pool_proj[l], w_attn_proj[l],
             w_out[l], norm_ffn2[l], ffn2_w_in[l], ffn2_w_out[l])
        xp, a, b, c = layer(xp, None, p)
        pp.append(a); kp.append(b); vp.append(c)
        xs, a, b, c = layer(xs, (state_pool[l], cache_k[l], cache_v[l]), p)
        ps.append(a); ksm.append(b); vsm.append(c)
    return (xp, xs, jnp.stack(pp), jnp.stack(kp), jnp.stack(vp), jnp.stack(ps), jnp.stack(ksm), jnp.stack(vsm))
```

```python
import numpy as np
import concourse.bass as bass
import concourse.mybir as mybir
from concourse.bass_utils import run_bass_kernel_spmd

F32 = mybir.dt.float32
BF16 = mybir.dt.bfloat16
AF = mybir.ActivationFunctionType
ALU = mybir.AluOpType

D = 1024
KC = 8
DFF = 2816
NJ = 22
NJP = 11
DIN = 3840
EPS = 1e-6
NCOL = 128
C_GQ, C_GK, C_BGA, C_BGB, C_PSC, C_PB, C_SINK, C_MASK, C_ZERO, C_INV, C_EPS = 0, 1, 2, 10, 18, 22, 26, 34, 35, 40, 36


class Res:
    __slots__ = ("name", "w", "rd", "excl")

    def __init__(self, name, excl=False):
        self.name = name
        self.w = None
        self.rd = {}
        self.excl = excl


class Op:
    __slots__ = ("eng", "dma", "sig", "cnt", "sem", "semval")


class Prog:
    def __init__(self, nc, ndma=10):
        self.nc = nc
        self.h = {"pe": nc.tensor, "act": nc.scalar, "dve": nc.vector, "pool": nc.gpsimd, "sp": nc.sync}
        self.ndma = ndma
        self.last = {}
        self.csem = {e: nc.alloc_semaphore("c_" + e) for e in ("pe", "act", "dve", "pool")}
        self.ccount = {e: 0 for e in self.csem}
        self.dpool = {q: [[nc.alloc_semaphore("d_%s%d" % (q, i)), 0] for i in range(ndma)] for q in ("sp", "pool")}
        self.dnext = {"sp": 0, "pool": 0}
        self.seen = {e: {} for e in self.h}
        self.pending = {e: [] for e in self.h}
        self.nops = 0

    def defer_start(self, eng):
        self.defer_eng, self.deferred = eng, []

    def defer_stop(self):
        d = self.deferred
        self.defer_eng, self.deferred = None, None
        return d

    def add(self, eng, fn, reads=(), writes=(), dma=False, extra=(), sig=True):
        if getattr(self, "defer_eng", None) == eng and not dma:
            self.deferred.append(lambda: self.add(eng, fn, reads, writes, dma=dma, extra=extra, sig=sig))
            return None
        o = Op()
        o.eng, o.dma, o.sig, o.cnt, o.sem, o.semval = eng, dma, (sig or dma), 0, None, 0
        self.nops += 1
        deps = set(extra)
        for r in reads:
            if r.w is not None:
                deps.add(r.w)
            if r.excl:
                for rd in r.rd.values():
                    if rd.eng != eng:
                        deps.add(rd)
        for w in writes:
            if w.w is not None and (dma or w.w.dma or w.w.eng != eng or eng != "pe"):
                deps.add(w.w)
            for rd in w.rd.values():
                if rd is not o and (dma or rd.dma or rd.eng != eng or eng != "pe"):
                    deps.add(rd)
        E = self.h[eng]
        waits = {}
        for d in deps:
            if d.dma:
                s, v = d.sem, d.semval
            else:
                if not d.sig:
                    assert d.eng == eng, "cross-engine dependency on unsignalled op"
                    continue
                s, v = self.csem[d.eng], d.cnt
            assert v > 0
            if waits.get(s.num, (None, 0))[1] < v:
                waits[s.num] = (s, v)
        slot = None
        if dma:
            slot = self.dpool[eng][self.dnext[eng] % self.ndma]
            self.dnext[eng] += 1
            if slot[1] > 0 and waits.get(slot[0].num, (None, 0))[1] < slot[1]:
                waits[slot[0].num] = (slot[0], slot[1])
        sn = self.seen[eng]
        for k in sorted(waits):
            s, v = waits[k]
            if sn.get(k, 0) >= v:
                continue
            E.wait_ge(s, v)
            sn[k] = v
        inst = fn()
        if dma:
            slot[1] += 16
            inst.then_inc(slot[0], 16)
            o.sem, o.semval = slot[0], slot[1]
        elif o.sig:
            self.ccount[eng] += 1
            inst.then_inc(self.csem[eng], 1)
            o.cnt = self.ccount[eng]
        key = ("dma", self.nops) if dma else eng
        for r in reads:
            r.rd[key] = o
        for w in writes:
            w.w = o
            w.rd = {}
        if not dma:
            if o.sig:
                for (res, kind, po) in self.pending[eng]:
                    if kind == "r" and res.rd.get(eng) is po:
                        res.rd[eng] = o
                    if kind == "w" and res.w is po:
                        res.w = o
                self.pending[eng] = []
                self.last[eng] = o
            else:
                for r in reads:
                    self.pending[eng].append((r, "r", o))
                for w in writes:
                    self.pending[eng].append((w, "w", o))
        return o

    def pe(self, fn, r=(), w=(), **k):
        return self.add("pe", fn, r, w, **k)

    def act(self, fn, r=(), w=(), **k):
        return self.add("act", fn, r, w, **k)

    def dve(self, fn, r=(), w=(), **k):
        return self.add("dve", fn, r, w, **k)

    def gps(self, fn, r=(), w=(), **k):
        return self.add("pool", fn, r, w, **k)

    def dma(self, q, fn, r=(), w=(), **k):
        return self.add(q, fn, r, w, dma=True, **k)

    def barrier(self):
        for e in ("pe", "act", "dve"):
            assert not self.pending[e], "barrier with unsignalled ops pending on " + e
        lasts = [self.last[e] for e in ("pe", "act", "dve") if e in self.last]
        for e in ("pe", "act", "dve"):
            ex = [o for o in lasts if o.eng != e]
            self.add(e, (lambda E=self.h[e]: E.nop()), extra=ex, sig=False)
        return lasts

    def phase_guard(self):
        for e in ("pe", "act", "dve", "pool"):
            assert not self.pending[e], "phase guard with unsignalled ops pending on " + e
        return [self.last[e] for e in ("pe", "act", "dve", "pool") if e in self.last]

    def emit(self):
        pass


def sg_split(nt_main):
    nt = nt_main + 2
    nsg = (nt + 5) // 6
    base, rem = divmod(nt, nsg)
    out, t = [], 0
    for i in range(nsg):
        n = base + (1 if i < rem else 0)
        out.append(list(range(t, t + n)))
        t += n
    return out


def groups_of(c0, c1):
    g = []
    while c0 < c1:
        n = min(512, c1 - c0)
        g.append((c0, n))
        c0 += n
    return g


class _Stop(Exception):
    pass


def build(nt_main, stop=None):
    def chk(name):
        if stop == name:
            raise _Stop()

    nc = bass.Bass("TRN2", target_bir_lowering=False)
    NT = nt_main + 2
    SGS = sg_split(nt_main)
    T_MAIN = nt_main * 128
    SAMP_TILE = nt_main + 1
    LAST_MAIN = nt_main

    def din(name, shape):
        return nc.dram_tensor(name, list(shape), F32, kind="ExternalInput").ap()

    def dout(name, shape):
        return nc.dram_tensor(name, list(shape), F32, kind="ExternalOutput").ap()

    xin = din("xin", [NT * 128, D])
    spool = din("spool", [2, 15, 512])
    ck = din("ck", [2, 128, 128])
    cv = din("cv", [2, 128, 128])
    f1wi = din("f1wi", [D, 2 * DFF])
    f1wo = din("f1wo", [DFF, D])
    wmix = din("wmix", [D, DIN])
    poolw = din("poolw", [4, 128, 128])
    wpp = din("wpp", [512, D])
    wap = din("wap", [D, D])
    wom = din("wom", [D, D])
    f2wi = din("f2wi", [D, 2 * DFF])
    f2wo = din("f2wo", [DFF, D])
    g3 = din("g3", [3, 128, D])
    colv = din("colv", [128, NCOL])
    cst = din("cst", [128, 4 * 128])

    y_main = dout("y_main", [T_MAIN, D])
    y_samp = dout("y_samp", [64, D])
    o_pool_p = dout("o_pool_p", [15, 512])
    o_k_p = dout("o_k_p", [128, 128])
    o_v_p = dout("o_v_p", [128, 128])
    o_pool_s = dout("o_pool_s", [2, 15, 512])
    o_k_s = dout("o_k_s", [2, 128, 128])
    o_v_s = dout("o_v_s", [2, 128, 128])

    cur = [((nc.sbuf_base + 63) // 64) * 64]

    def alloc(name, shape, dt, at=None):
        esz = 4 if dt == F32 else 2
        n = esz
        for s in shape[1:]:
            n *= s
        n = ((n + 63) // 64) * 64
        if at is None:
            off = cur[0]
            cur[0] += n
        else:
            off = at
        assert off + n <= nc.sbuf_top, (name, off, n, nc.sbuf_top)
        return nc.alloc_sbuf_tensor_at(name, list(shape), dt, offset=off), off + n

    def A_(name, shape, dt):
        return alloc(name, shape, dt)[0]

    TS = 768
    X = A_("X", [128, 6, D], F32)
    wst = [A_("wst%d" % i, [128, KC, 512], BF16) for i in range(3)]
    Wpp = A_("Wpp", [128, 4, D], BF16)
    Wap = A_("Wap", [128, 8, D], BF16)
    Wom = A_("Wom", [128, 8, D], BF16)
    Wpw = A_("Wpw", [128, 4, 128], BF16)
    Wk = A_("Wk", [128, KC, 128], BF16)
    Wv = A_("Wv", [128, KC, 128], BF16)
    hT = A_("hT", [128, KC, TS], BF16)
    gt = A_("gt", [128, D], F32)
    htile_off = cur[0]
    htile = [A_("htile%d" % i, [128, D], BF16) for i in range(2)]
    stmp_off = cur[0]
    stmp = [A_("stmp%d" % i, [128, 512], BF16) for i in range(2)]
    htile.append(nc.alloc_sbuf_tensor_at("htile2", [128, D], BF16, offset=stmp_off))
    cb = A_("cb", [128, 4, 128], BF16)
    id32 = A_("id32", [128, 128], F32)
    cv_ = A_("colv", [128, NCOL], F32)
    ss = A_("ss", [128, 64], F32)
    rs = A_("rs", [128, 64], F32)
    bsc = A_("bsc", [128, 4], F32)
    esk = A_("esk", [128, 8], F32)
    osb = A_("osb", [128, 512], F32)
    osb2 = A_("osb2", [128, 512], F32)
    kfull = A_("kfull", [128, 128], F32)
    kfull2 = A_("kfull2", [128, 128], F32)
    fz = A_("fz", [128, 4], F32)
    kT_hist = A_("kT_hist", [128, 4, 128], BF16)
    V_hist = A_("V_hist", [128, 2, 2, 128], BF16)
    u_hist = A_("u_hist", [128, 4, 16], F32)
    phase0 = cur[0]
    Abuf, e1 = alloc("Abuf", [128, NJ, TS], BF16, at=phase0)
    Wo, e2 = alloc("Wo", [128, NJ, D], BF16, at=e1)
    mo = [phase0]

    def M_(name, shape, dt):
        t, e = alloc(name, shape, dt, at=mo[0])
        mo[0] = e
        return t

    qT_off = mo[0]
    qT = M_("qT", [128, TS // 64, 8, 64], BF16)
    uT = M_("uT", [128, 4, 16 + TS], F32)
    Sa = M_("Sa", [128, 16 + TS], F32)
    Sb = M_("Sb", [128, 16 + TS], F32)
    pooled_off = mo[0]
    pooled = M_("pooled", [128, 4, TS], BF16)
    yaT = M_("yaT", [128, 4, TS], BF16)
    ybT = M_("ybT", [128, 8, TS], BF16)
    kTp = M_("kTp", [128, 4, 128 + TS], BF16)
    Vbd = M_("Vbd", [128, 14, 2, 128], BF16)
    PT = [M_("PT%d" % i, [128, 768], BF16) for i in range(2)]
    den = [M_("den%d" % i, [128, 256], F32) for i in range(2)]
    sq = [M_("sq%d" % i, [128, 512], BF16) for i in range(2)]
    rstd_off = mo[0]
    rstd = [M_("rstd%d" % i, [128, 512], F32) for i in range(2)]
    gtmp_off = mo[0]
    gtmp = [M_("gtmp0", [128, 512], F32)] * 2
    kcp = nc.alloc_sbuf_tensor_at("kcp", [128, 2, 4, 128], BF16, offset=stmp_off)
    Vs = nc.alloc_sbuf_tensor_at("Vs", [128, 2, 3, 2, 128], BF16, offset=htile_off)
    ksT = nc.alloc_sbuf_tensor_at("ksT", [128, 2, 4, 128], BF16, offset=rstd_off + 2048)
    usT = M_("usT", [128, 4, 96], F32)
    sp16 = nc.alloc_sbuf_tensor_at("sp16", [16, 2, 512], F32, offset=rstd_off)

    mT = nc.alloc_sbuf_tensor_at("mT", [128, 8, TS], BF16, offset=qT_off)
    tT = uT

    ps = nc.alloc_psum_tensor("ps", [128, 8, 512], F32)
    psb = ps.bitcast(BF16)

    P = Prog(nc)
    R = Res
    bank = [R("bank%d" % i, excl=True) for i in range(8)]
    rX = [R("X%d" % i) for i in range(6)]
    r_wst = [[R("wst%dg" % i), R("wst%du" % i)] for i in range(3)]
    r_perm = R("permw")
    r_c = R("consts")
    r_gt = R("gt")
    r_hTt = [R("hT%d" % i) for i in range(6)]
    norm_q = []

    def rh(c0, n):
        while norm_q and norm_q[0][1] < c0 + n:
            _n_stage2(*norm_q.pop(0))
        return r_hTt[c0 // 128:(c0 + n - 1) // 128 + 1]
    r_htile = [R("ht0"), R("ht1"), R("ht2")]
    r_stmp = [R("st0"), R("st1")]
    r_ss = R("ss")
    r_ssc = [R("ss%d" % i) for i in range(64)]
    r_rsc = [R("rs%d" % i) for i in range(64)]
    r_rs = R("rs")
    r_A = R("A")
    r_Wo = [R("Wo%d" % i) for i in range(4)]
    r_osb = R("osb")
    r_osb2 = R("osb2")
    r_kfull = R("kfull")

    T, V, S, G = nc.tensor, nc.vector, nc.scalar, nc.gpsimd
    out_dmas = []

    P.dma("sp", lambda: nc.sync.dma_start(out=cv_[:], in_=colv), w=[r_c])
    P.dma("sp", lambda: nc.sync.dma_start(out=id32[:], in_=cst[:, 0:128]), w=[r_c])
    P.dma("pool", lambda: nc.gpsimd.dma_start(out=cb[:], in_=cst.rearrange("p (a b) -> p a b", a=4)), w=[r_c])
    P.dve(lambda: V.memset(ss[:], 0.0), w=[r_ss] + r_ssc)
    r_Wk = R("Wk")
    r_Wv, r_Wpw, r_Wpp, r_Wap, r_Wom = R("Wv"), R("Wpw"), R("Wpp"), R("Wap"), R("Wom")
    wq_k0 = 512 + 1024

    def load_perm():
        P.dma("pool", lambda: nc.gpsimd.dma_start(
            out=Wk[:], in_=wmix[:, wq_k0: wq_k0 + 128].rearrange("(kc q) n -> q kc n", q=128)), w=[r_Wk])
        P.dma("pool", lambda: nc.gpsimd.dma_start(
            out=Wv[:], in_=wmix[:, wq_k0 + 128: wq_k0 + 256].rearrange("(kc q) n -> q kc n", q=128)), w=[r_Wv])
        P.dma("pool", lambda: nc.gpsimd.dma_start(out=Wpw[:], in_=poolw.rearrange("g c d -> c g d")), w=[r_Wpw])
        P.dma("pool", lambda: nc.gpsimd.dma_start(out=Wpp[:], in_=wpp.rearrange("(g q) n -> q g n", q=128)), w=[r_Wpp])
        P.dma("pool", lambda: nc.gpsimd.dma_start(out=Wap[:], in_=wap.rearrange("(g q) n -> q g n", q=128)), w=[r_Wap])
        P.dma("pool", lambda: nc.gpsimd.dma_start(out=Wom[:], in_=wom.rearrange("(g q) n -> q g n", q=128)), w=[r_Wom])

    P.dve(lambda: V.tensor_tensor(out=bsc[:], in0=cv_[:, C_PB:C_PB + 4], in1=cv_[:, C_PSC:C_PSC + 4], op=ALU.mult),
          r=[r_c], w=[r_c])
    P.act(lambda: S.activation(out=esk[:], in_=cv_[:, C_SINK:C_SINK + 8], func=AF.Exp), r=[r_c], w=[r_c])
    for s in range(2):
        out_dmas.append(P.dma("sp", lambda s=s: nc.sync.dma_start(out=o_k_s[s, 0:96, :], in_=ck[s, 32:128, :])))
        out_dmas.append(P.dma("sp", lambda s=s: nc.sync.dma_start(out=o_v_s[s, 0:96, :], in_=cv[s, 32:128, :])))

    blocks = []
    for sg in range(len(SGS)):
        for jp in range(NJP):
            blocks.append(("f1", jp))
        for b in range(7):
            blocks.append(("mx", b))
        for jp in range(NJP):
            blocks.append(("f2", jp))
    st = {"issued": 0, "pos": 0}

    def issue_block(i):
        kind, a = blocks[i]
        buf = wst[i % 3]
        rb = r_wst[i % 3]
        if kind in ("f1", "f2"):
            wsrc = f1wi if kind == "f1" else f2wi
            v = wsrc.rearrange("(kc q) n -> q kc n", q=128)
            P.dma("pool", lambda: nc.gpsimd.dma_start(out=buf[:, :, 0:256], in_=v[:, :, 256 * a:256 * a + 256]), w=[rb[0]])
            P.dma("pool", lambda: nc.gpsimd.dma_start(out=buf[:, :, 256:512],
                                                      in_=v[:, :, DFF + 256 * a:DFF + 256 * a + 256]), w=[rb[1]])
        else:
            v = wmix.rearrange("(kc q) n -> q kc n", q=128)
            cols = [0, 512, 1024, 1792, 2816, 2304, 3328][a]
            P.dma("pool", lambda: nc.gpsimd.dma_start(out=buf[:], in_=v[:, :, cols:cols + 512]), w=rb)

    def get_block(look=3):
        i = st["pos"]
        while st["issued"] < min(i + look, len(blocks)):
            issue_block(st["issued"])
            st["issued"] += 1
        st["pos"] += 1
        return wst[i % 3], r_wst[i % 3], []

    cnt = {"norm": 0, "gu": 0, "wo": 0, "tr": 0, "pb": 0}

    def norm_stage(tl):
        norm_begin(len(tl))
        for (li, lc) in tl:
            norm_push(li, lc, 6)

    norm_seq = {"off": 0, "i": 0}

    def norm_begin(m):
        assert not norm_q
        norm_seq["off"] = (2 - m) % 3
        norm_seq["i"] = 0

    def _n_stage1(li):
        k = cnt["norm"]
        cnt["norm"] += 1
        hb = (norm_seq["i"] + norm_seq["off"]) % 3
        norm_seq["i"] += 1
        ht, rht = htile[hb], r_htile[hb]
        alias = r_stmp if hb == 2 else []
        col = k % 64
        P.act(lambda: S.activation(out=ht[:], in_=X[:, li, :], func=AF.Square, accum_out=ss[:, col:col + 1]),
              r=[rX[li], r_ssc[col]], w=[rht, r_ssc[col]] + alias)
        P.act(lambda: S.activation(out=rs[:, col:col + 1], in_=ss[:, col:col + 1], func=AF.Sqrt, scale=1.0 / D,
                                   bias=cv_[:, C_EPS:C_EPS + 1]), r=[r_ssc[col], r_c], w=[r_rsc[col]])
        P.dve(lambda: V.reciprocal(out=rs[:, col:col + 1], in_=rs[:, col:col + 1]), r=[r_rsc[col]], w=[r_rsc[col]])
        P.dve(lambda: V.scalar_tensor_tensor(out=ht[:], in0=X[:, li, :], scalar=rs[:, col:col + 1], in1=gt[:],
                                             op0=ALU.mult, op1=ALU.mult), r=[rX[li], r_rsc[col], r_gt], w=[rht] + alias)
        return (k, hb)

    def _n_stage2(kh, lc, tb0):
        k, hb = kh
        ht, rht = htile[hb], r_htile[hb]
        b = tb0 + (k % 2)
        for kc in range(KC):
            P.pe(lambda: T.transpose(out=psb[:, b, kc * 128:(kc + 1) * 128], in_=ht[:, kc * 128:(kc + 1) * 128],
                                     identity=cb[:, 0, :]), r=[rht, r_c], w=[bank[b]], sig=(kc == KC - 1))
        P.dve(lambda: V.tensor_copy(out=hT[:, :, lc:lc + 128], in_=psb[:, b, :].rearrange("p (k c) -> p k c", k=KC)),
              r=[bank[b]], w=[r_hTt[lc // 128]])

    def norm_push(li, lc, tb0):
        while len(norm_q) >= 2:
            _n_stage2(*norm_q.pop(0))
        k = _n_stage1(li)
        norm_q.append((k, lc, tb0))

    def norm_flush():
        while norm_q:
            _n_stage2(*norm_q.pop(0))

    def norm_flush_alias():
        idx = [i for i, e in enumerate(norm_q) if e[0][1] == 2]
        if idx:
            for _ in range(idx[-1] + 1):
                _n_stage2(*norm_q.pop(0))

    def load_g(i):
        P.dma("sp", lambda: nc.sync.dma_start(out=gt[:], in_=g3[i]), w=[r_gt])

    prefetched = set()

    def load_x(sg_i, li):
        t = SGS[sg_i][li]
        P.dma("sp", lambda: nc.sync.dma_start(out=X[:, li, :], in_=xin[t * 128:(t + 1) * 128, :]), w=[rX[li]])
        prefetched.add((sg_i, li))

    def ffn(sgi, which, tiles_li, store, pre_loop=None, hook=None, guard=False, hook_lag=0):
        wo_src = f1wo if which == 1 else f2wo
        if store and sgi + 1 < len(SGS):
            done_li = set(li for (li, _) in tiles_li)
            for li in range(len(SGS[sgi + 1])):
                if li not in done_li:
                    load_x(sgi + 1, li)
        norm_flush_alias()
        lasts = P.phase_guard() if guard else []
        st_guard = {"a": list(lasts)}
        c0 = tiles_li[0][0] * 128
        c1 = (tiles_li[-1][0] + 1) * 128
        if tiles_li[-1][1] == SAMP_TILE:
            c1 -= 64
            P.dve(lambda: V.memset(Abuf[:, :, c1:c1 + 64], 0.0), w=[r_A], extra=st_guard["a"])
            st_guard["a"] = []
        grps = groups_of(c0, c1)
        def fetch(jp, look=3):
            buf, rb, _ = get_block(look)
            if jp == 2 and sgi == 0 and which == 1:
                load_perm()
            if jp in (1, 3, 5, 7):
                qd = (jp - 1) // 2
                j0, j1 = [(0, 6), (6, 12), (12, 17), (17, 22)][qd]
                P.dma("pool", lambda: nc.gpsimd.dma_start(
                    out=Wo[:, j0:j1, :], in_=wo_src[j0 * 128:j1 * 128, :].rearrange("(j q) n -> q j n", q=128)),
                    w=[r_Wo[qd]], extra=lasts)
            return buf, rb

        def gate_up(jp, sub, buf, rb, g0, n):
            j = 2 * jp + sub
            k = cnt["gu"]
            cnt["gu"] += 1
            bg, bu = 2 * (k % 2), 2 * (k % 2) + 1
            for half, bk in ((0, bg), (1, bu)):
                for kc in range(KC):
                    P.pe(lambda: T.matmul(ps[:, bk, 0:n], lhsT=buf[:, kc, 256 * half + 128 * sub:256 * half + 128 * sub + 128],
                                          rhs=hT[:, kc, g0:g0 + n], start=(kc == 0), stop=(kc == KC - 1)),
                         r=[rb[half]] + rh(g0, n), w=[bank[bk]], sig=(kc == KC - 1))
            sb_, rsb = stmp[k % 2], r_stmp[k % 2]
            P.act(lambda: S.activation(out=sb_[:, 0:n], in_=ps[:, bg, 0:n], func=AF.Silu), r=[bank[bg]], w=[rsb])
            P.dve(lambda: V.tensor_tensor(out=Abuf[:, j, g0:g0 + n], in0=ps[:, bu, 0:n], in1=sb_[:, 0:n], op=ALU.mult),
                  r=[bank[bu], rsb], w=[r_A], extra=st_guard["a"])
            st_guard["a"] = []

        blk0 = fetch(0)
        blk1 = fetch(1, look=2)
        for gi, (g0, n) in enumerate(grps):
            for jp, (buf, rb) in ((0, blk0), (1, blk1)):
                for sub in range(2):
                    gate_up(jp, sub, buf, rb, g0, n)
        norm_flush()
        for jp in range(2, NJP):
            buf, rb = fetch(jp)
            for sub in range(2):
                for (g0, n) in grps:
                    gate_up(jp, sub, buf, rb, g0, n)
        if pre_loop is not None:
            pre_loop()
        hookq = []
        for (li, gt_) in tiles_li:
            k = cnt["wo"]
            cnt["wo"] += 1
            b0 = 4 + 2 * (k % 2)
            for nh in range(2):
                for j in range(NJ):
                    qd = 0 if j < 6 else 1 if j < 12 else 2 if j < 17 else 3
                    P.pe(lambda j=j, nh=nh: T.matmul(ps[:, b0 + nh, :], lhsT=Abuf[:, j, li * 128:li * 128 + 128],
                                                     rhs=Wo[:, j, nh * 512:nh * 512 + 512], start=(j == 0),
                                                     stop=(j == NJ - 1)), r=[r_A, r_Wo[qd]], w=[bank[b0 + nh]], sig=(j == NJ - 1))
            P.dve(lambda: V.scalar_tensor_tensor(out=X[:, li, :], in0=ps[:, b0:b0 + 2, :].rearrange("p a b -> p (a b)"),
                                                 scalar=0.5, in1=X[:, li, :], op0=ALU.mult, op1=ALU.add),
                  r=[bank[b0], bank[b0 + 1], rX[li]], w=[rX[li]])
            if store:
                if gt_ == SAMP_TILE:
                    out_dmas.append(P.dma("sp", lambda: nc.sync.dma_start(out=y_samp, in_=X[0:64, li, :]), r=[rX[li]]))
                else:
                    out_dmas.append(P.dma("sp", lambda gt_=gt_: nc.sync.dma_start(
                        out=y_main[(gt_ - 1) * 128:gt_ * 128, :], in_=X[:, li, :]), r=[rX[li]]))
                if sgi + 1 < len(SGS) and li < len(SGS[sgi + 1]):
                    load_x(sgi + 1, li)
            if hook is not None:
                hookq.append(li)
                while len(hookq) > hook_lag:
                    hook(hookq.pop(0))

        while hook is not None and hookq:
            hook(hookq.pop(0))

    r_qT, r_uT, r_Sa, r_Sb, r_pooled, r_yaT, r_ybT = R("qT"), R("uT"), R("Sa"), R("Sb"), R("pooled"), R("yaT"), R("ybT")
    r_kTp, r_Vbd = R("kTp"), R("Vbd")
    r_PT, r_den, r_sq, r_rstd, r_gtmp = [R("PT0"), R("PT1")], [R("den0"), R("den1")], [R("sq0"), R("sq1")], \
        [R("rstd0"), R("rstd1")], [R("gt0")] * 2
    r_samp = R("samp")
    r_hist = R("hist")
    r_usT = R("usT")

    qk_pend = []

    def qk_flush(keep=0):
        while len(qk_pend) > keep:
            qk_pend.pop(0)()

    def qk_norm(bk, n, gcol, emit_out):
        k = cnt["pb"]
        cnt["pb"] += 1
        sq_, rsq = sq[k % 2], r_sq[k % 2]
        rd_, rrd = rstd[k % 2], r_rstd[k % 2]
        mb = 4 + (k % 2)
        P.act(lambda: S.activation(out=sq_[:, 0:n], in_=ps[:, bk, 0:n], func=AF.Square), r=[bank[bk]], w=[rsq])

        def part_b():
            P.pe(lambda: T.matmul(ps[:, mb, 0:n], lhsT=cb[:, 2, :], rhs=sq_[:, 0:n], start=True, stop=True),
                 r=[rsq, r_c], w=[bank[mb]])
            P.act(lambda: S.activation(out=rd_[:, 0:n], in_=ps[:, mb, 0:n], func=AF.Ln, bias=cv_[:, C_EPS:C_EPS + 1]),
                  r=[bank[mb], r_c], w=[rrd])
            P.act(lambda: S.activation(out=rd_[:, 0:n], in_=rd_[:, 0:n], func=AF.Exp, scale=-0.5), r=[rrd], w=[rrd])
            emit_out(rd_, rrd)
        qk_flush(0)
        qk_pend.append(part_b)

    def mix(sgi, tiles, pre_out=None, hook=None):
        nt_sg = len(tiles)
        first_main_li = 1 if tiles[0] == 0 else 0
        has_samp = tiles[-1] == SAMP_TILE
        lasts = P.phase_guard()
        P.dve(lambda: V.memset(fz[:, 0:1], 0.0), extra=lasts, sig=False)
        P.act(lambda: S.copy(out=fz[:, 1:2], in_=cv_[:, C_ZERO:C_ZERO + 1]), extra=lasts, sig=False)
        P.gps(lambda: G.memset(fz[:, 2:3], 0.0), extra=lasts, sig=False)
        if sgi == 0:
            P.gps(lambda: G.memset(Vbd[:], 0.0), w=[r_Vbd])
            P.gps(lambda: G.memset(PT[0][:], 0.0), w=[r_PT[0]])
            P.gps(lambda: G.memset(PT[1][:], 0.0), w=[r_PT[1]])
            P.dve(lambda: V.memset(uT[:, :, 0:16], 0.0), w=[r_uT])
            P.dve(lambda: V.memset(kTp[:, :, 0:128], 0.0), w=[r_kTp])
        else:
            P.gps(lambda: G.memset(Vbd[:], 0.0), w=[r_Vbd])
            P.gps(lambda: G.memset(PT[0][:], 0.0), w=[r_PT[0]])
            P.gps(lambda: G.memset(PT[1][:], 0.0), w=[r_PT[1]])
            P.act(lambda: S.copy(out=kTp[:, :, 0:128], in_=kT_hist[:]), r=[r_hist], w=[r_kTp])
            P.act(lambda: S.copy(out=Vbd[:, 0:2, :, :], in_=V_hist[:]), r=[r_hist], w=[r_Vbd])
            P.act(lambda: S.copy(out=uT[:, :, 0:16], in_=u_hist[:]), r=[r_hist], w=[r_uT])
        dks = []
        if has_samp:
            norm_flush()
            P.dve(lambda: V.memset(kcp[:], 0.0), w=[r_samp] + r_htile + r_stmp)
            P.dve(lambda: V.memset(Vs[:], 0.0), w=[r_samp] + r_htile + r_stmp)
            dks = []
            for s in range(2):
                for h in range(2):
                    for p in range(2):
                        dks.append(P.dma("pool", lambda s=s, h=h, p=p: nc.gpsimd.dma_start(
                            out=kcp[:, s, 2 * h + p, 64 * p:64 * p + 64], in_=ck[s, :, 64 * h:64 * h + 64]),
                            r=[], w=[], extra=[r_samp.w]))
                for blk in range(2):
                    for p in range(2):
                        dks.append(P.dma("pool", lambda s=s, blk=blk, p=p: nc.gpsimd.dma_start(
                            out=Vs[64 * p:64 * p + 64, s, blk, :, 64 * p:64 * p + 64],
                            in_=cv[s, 64 * blk:64 * blk + 64, :].rearrange("k (h d) -> k h d", h=2)),
                            r=[], w=[], extra=[r_samp.w]))
        chk('mnorm')
        CT = nt_sg * 128
        grps_all = groups_of(0, CT)
        def v_proj(li_lo, li_hi):
            for li, t in list(enumerate(tiles))[li_lo:li_hi]:
                k = cnt["gu"]
                cnt["gu"] += 1
                bk = k % 4
                for kc in range(KC):
                    P.pe(lambda: T.matmul(ps[:, bk, 0:128], lhsT=hT[:, kc, li * 128:li * 128 + 128], rhs=Wv[:, kc, :],
                                          start=(kc == 0), stop=(kc == KC - 1)),
                         r=[r_Wv] + rh(li * 128, 128), w=[bank[bk]], sig=(kc == KC - 1))
                for c2 in range(2):
                    ch = 2 + 2 * li + c2
                    for p in range(2):
                        P.act(lambda: S.copy(
                            out=Vbd[64 * p:64 * p + 64, ch, :, 64 * p:64 * p + 64],
                            in_=ps[64 * c2:64 * c2 + 64, bk, 0:128].rearrange("q (h d) -> q h d", h=2)),
                            r=[bank[bk]], w=[r_Vbd])
                if t == LAST_MAIN:
                    P.dve(lambda: V.tensor_copy(out=osb[:, 0:128], in_=ps[:, bk, 0:128]), r=[bank[bk]], w=[r_osb])
                    out_dmas.append(P.dma("sp", lambda: nc.sync.dma_start(out=o_v_p, in_=osb[:, 0:128]), r=[r_osb]))
                if t == SAMP_TILE:
                    P.dve(lambda: V.tensor_copy(out=osb[0:64, 256:384], in_=ps[0:64, bk, 0:128]), r=[bank[bk]], w=[r_osb])
                    for s in range(2):
                        out_dmas.append(P.dma("sp", lambda s=s: nc.sync.dma_start(
                            out=o_v_s[s, 96:128, :], in_=osb[32 * s:32 * s + 32, 256:384]), r=[r_osb]))

        P.dve(lambda: V.memset(kTp[:, :, 128:128 + CT], 0.0), w=[r_kTp])
        for gi_k, (g0, n) in enumerate(grps_all):
            bk = 6 + (gi_k % 2)
            for kc in range(KC):
                P.pe(lambda: T.matmul(ps[:, bk, 0:n], lhsT=Wk[:, kc, :], rhs=hT[:, kc, g0:g0 + n], start=(kc == 0),
                                      stop=(kc == KC - 1)), r=[r_Wk] + rh(g0, n), w=[bank[bk]], sig=(kc == KC - 1))

            def emit_k(rd_, rrd, bk=bk, g0=g0, n=n):
                for h in range(2):
                    for p in range(2):
                        P.dve(lambda: V.scalar_tensor_tensor(
                            out=kTp[64 * p:64 * p + 64, 2 * h + p, 128 + g0:128 + g0 + n],
                            in0=ps[64 * h:64 * h + 64, bk, 0:n], scalar=cv_[64 * h:64 * h + 64, C_GK:C_GK + 1],
                            in1=rd_[64 * h:64 * h + 64, 0:n], op0=ALU.mult, op1=ALU.mult),
                            r=[bank[bk], rrd, r_c], w=[r_kTp])
                    for li, t in enumerate(tiles):
                        if t in (LAST_MAIN, SAMP_TILE) and g0 <= li * 128 < g0 + n:
                            o = li * 128 - g0
                            P.dve(lambda: V.scalar_tensor_tensor(
                                out=kfull_t[t][64 * h:64 * h + 64, :],
                                in0=ps[64 * h:64 * h + 64, bk, o:o + 128], scalar=cv_[64 * h:64 * h + 64, C_GK:C_GK + 1],
                                in1=rd_[64 * h:64 * h + 64, o:o + 128], op0=ALU.mult, op1=ALU.mult),
                                r=[bank[bk], rrd, r_c], w=[r_kf[t]])
            qk_norm(bk, n, C_GK, emit_k)
            v_proj(g0 // 128, (g0 + n) // 128)
        qk_flush(0)
        chk('kproj')
        if has_samp:
            P.dma("pool", lambda: nc.gpsimd.dma_start(out=sp16[0:15, :, :], in_=spool.rearrange("s r f -> r s f")),
                  w=[r_samp] + r_rstd, extra=lasts)
        chk('kv')
        buf, rb, extra = get_block()
        for g in range(4):
            for (g0, n) in grps_all:
                k = cnt["gu"]
                cnt["gu"] += 1
                bk = k % 4
                for kc in range(KC):
                    P.pe(lambda kc=kc, g=g, bk=bk: T.matmul(ps[:, bk, 0:n], lhsT=buf[:, kc, 128 * g:128 * g + 128],
                                                            rhs=hT[:, kc, g0:g0 + n], start=(kc == 0), stop=(kc == KC - 1)),
                         r=rb + rh(g0, n), w=[bank[bk]], extra=extra, sig=(kc == KC - 1))
                P.act(lambda g=g, bk=bk: S.copy(out=uT[:, g, 16 + g0:16 + g0 + n], in_=ps[:, bk, 0:n]),
                      r=[bank[bk]], w=[r_uT])
        def pool_group(U, g, L, rU):
            P.gps(lambda: G.tensor_tensor(out=Sa[:, 1:L], in0=U[:, g, 1:L], in1=U[:, g, 0:L - 1], op=ALU.add),
                  r=[rU], w=[r_Sa])
            if g >= 1:
                P.gps(lambda: G.tensor_tensor(out=Sb[:, 3:L], in0=Sa[:, 3:L], in1=Sa[:, 1:L - 2], op=ALU.add),
                      r=[r_Sa], w=[r_Sb])
            if g >= 2:
                P.gps(lambda: G.tensor_tensor(out=Sa[:, 7:L], in0=Sb[:, 7:L], in1=Sb[:, 3:L - 4], op=ALU.add),
                      r=[r_Sb], w=[r_Sa])
            if g >= 3:
                P.gps(lambda: G.tensor_tensor(out=Sb[:, 15:L], in0=Sa[:, 15:L], in1=Sa[:, 7:L - 8], op=ALU.add),
                      r=[r_Sa], w=[r_Sb])
            return (Sa, r_Sa) if g in (0, 2) else (Sb, r_Sb)

        n_pool_tiles = nt_sg - (1 if has_samp else 0)
        L = 16 + n_pool_tiles * 128
        c_lo = first_main_li * 128
        c_hi = n_pool_tiles * 128
        def pool_main_group(g):
            w_ = float(2 ** (g + 1))
            Sg_, rSg_ = pool_group(uT, g, L, r_uT)
            P.gps(lambda: G.tensor_scalar(out=Sg_[:, 16 + c_lo:16 + c_hi], in0=Sg_[:, 16 + c_lo:16 + c_hi], scalar1=1.0 / w_,
                                          scalar2=None, op0=ALU.mult), r=[rSg_], w=[rSg_])
            if sgi == 0:
                P.gps(lambda: G.tensor_tensor(out=Sg_[:, 16 + 128:16 + 144], in0=Sg_[:, 16 + 128:16 + 144],
                                              in1=cv_[:, C_INV + 16 * g:C_INV + 16 * g + 16], op=ALU.mult),
                      r=[rSg_, r_c], w=[rSg_])
            P.gps(lambda: G.tensor_tensor(out=pooled[:, g, c_lo:c_hi], in0=Sg_[:, 16 + c_lo:16 + c_hi],
                                          in1=uT[:, g, 16 + c_lo:16 + c_hi], op=ALU.subtract),
                  r=[rSg_, r_uT], w=[r_pooled])

        P.defer_start("pool")
        for g in range(4):
            pool_main_group(g)
        pool_thunks = P.defer_stop()
        qblk0 = get_block()
        if has_samp:
            lis = nt_sg - 1
            P.dve(lambda: V.memset(usT[:], 0.0), w=[r_usT])
            for s in range(2):
                for g in range(4):
                    P.pe(lambda s=s, g=g: T.transpose(out=ps[:, 7, 16 * (4 * s + g):16 * (4 * s + g) + 15],
                                                      in_=sp16[0:15, s, 128 * g:128 * g + 128], identity=id32[0:15, 0:15]),
                         r=[r_samp, r_c] + r_rstd, w=[bank[7]], sig=(s == 1 and g == 3))
            for s in range(2):
                P.act(lambda s=s: S.copy(out=usT[:, :, 48 * s + 1:48 * s + 16],
                                         in_=ps[:, 7, 64 * s:64 * s + 64].rearrange("p (g c) -> p g c", g=4)[:, :, 0:15]),
                      r=[bank[7]], w=[r_usT])
                P.act(lambda s=s: S.copy(out=usT[:, :, 48 * s + 16:48 * s + 48],
                                         in_=uT[:, :, 16 + lis * 128 + 32 * s:16 + lis * 128 + 32 * s + 32]),
                      r=[r_uT], w=[r_usT])
            for g in range(4):
                w_ = float(2 ** (g + 1))
                Sg_, rSg_ = pool_group(usT, g, 96, r_usT)
                for s in range(2):
                    P.gps(lambda: G.tensor_scalar(out=Sg_[:, 48 * s + 16:48 * s + 48], in0=Sg_[:, 48 * s + 16:48 * s + 48],
                                                  scalar1=1.0 / w_, scalar2=None, op0=ALU.mult), r=[rSg_], w=[rSg_])
                    P.gps(lambda: G.tensor_tensor(out=pooled[:, g, lis * 128 + 32 * s:lis * 128 + 32 * s + 32],
                                                  in0=Sg_[:, 48 * s + 16:48 * s + 48],
                                                  in1=usT[:, g, 48 * s + 16:48 * s + 48], op=ALU.subtract),
                          r=[rSg_, r_usT], w=[r_pooled])
            P.dve(lambda: V.memset(pooled[:, :, lis * 128 + 64:lis * 128 + 128], 0.0), w=[r_pooled])
            for g in range(4):
                P.pe(lambda g=g: T.transpose(out=ps[:, 7, 128 * g:128 * g + 128],
                                             in_=uT[:, g, 16 + lis * 128:16 + lis * 128 + 128], identity=id32[:]),
                     r=[r_uT, r_c], w=[bank[7]], sig=(g == 3))
            P.act(lambda: S.copy(out=osb2[:], in_=ps[:, 7, :]), r=[bank[7]], w=[r_osb2])
            for s in range(2):
                out_dmas.append(P.dma("sp", lambda s=s: nc.sync.dma_start(out=o_pool_s[s], in_=osb2[32 * s + 17:32 * s + 32, :]),
                                      r=[r_osb2]))
        for qb in range(2):
            buf, rb, extra = qblk0 if qb == 0 else get_block()
            for c4 in range(4):
                m = 4 * qb + c4
                for (g0, n) in grps_all:
                    k = cnt["gu"]
                    cnt["gu"] += 1
                    bk = k % 4
                    for kc in range(KC):
                        P.pe(lambda kc=kc, c4=c4, bk=bk: T.matmul(ps[:, bk, 0:n], lhsT=buf[:, kc, 128 * c4:128 * c4 + 128],
                                                                  rhs=hT[:, kc, g0:g0 + n], start=(kc == 0),
                                                                  stop=(kc == KC - 1)),
                             r=rb + rh(g0, n), w=[bank[bk]], extra=extra, sig=(kc == KC - 1))

                    def emit_q(rd_, rrd, bk=bk, m=m, g0=g0, n=n):
                        P.dve(lambda: V.scalar_tensor_tensor(
                            out=qT[:, g0 // 64:(g0 + n) // 64, m, :],
                            in0=ps[:, bk, 0:n].rearrange("p (c q) -> p c q", q=64),
                            scalar=cv_[:, C_GQ:C_GQ + 1], in1=rd_[:, 0:n].rearrange("p (c q) -> p c q", q=64),
                            op0=ALU.mult, op1=ALU.mult), r=[bank[bk], rrd, r_c], w=[r_qT])
                    qk_norm(bk, n, C_GQ, emit_q)
                    if pool_thunks:
                        pool_thunks.pop(0)()

        qk_flush(0)
        chk('proj')
        for li, t in enumerate(tiles):
            if t == LAST_MAIN:
                for g in range(4):
                    P.pe(lambda g=g, li=li: T.transpose(out=ps[:, 7, 128 * g:128 * g + 128],
                                                        in_=uT[:, g, 16 + li * 128:16 + li * 128 + 128], identity=id32[:]),
                         r=[r_uT, r_c], w=[bank[7]], sig=(g == 3))
                P.act(lambda: S.copy(out=osb2[:], in_=ps[:, 7, :]), r=[bank[7]], w=[r_osb2])
                out_dmas.append(P.dma("sp", lambda: nc.sync.dma_start(out=o_pool_p, in_=osb2[113:128, :]), r=[r_osb2]))
        chk('pool')
        attn_pend = []

        def attn_flush():
            while attn_pend:
                attn_pend.pop(0)()

        def attn_block(keyblocks, q_ap_fn, nq, out_fn, h, sumw):
            k = cnt["wo"]
            cnt["wo"] += 1
            b0 = 3 * (k % 2)
            N = 4 * nq
            pt_, rpt = PT[k % 2], r_PT[k % 2]
            dn_, rdn = den[k % 2], r_den[k % 2]
            for i, (kfn, M, bcol, vap, rk, rv) in enumerate(keyblocks):
                bk = b0 + (i * N) // 512
                off = (i * N) % 512
                for p in range(2):
                    qo = 0
                    for (qap, qn) in q_ap_fn():
                        P.pe(lambda: T.matmul(ps[64 * p:64 * p + M, bk, off + qo:off + qo + qn], lhsT=kfn(p), rhs=qap,
                                              start=True, stop=True), r=[rk, r_qT], w=[bank[bk]])
                        qo += qn
            psf = ps[:, b0:b0 + 2, :].rearrange("p a b -> p (a b)")
            i = 0
            nkb = len(keyblocks)
            while i < nkb:
                M, bcol = keyblocks[i][1], keyblocks[i][2]
                if M == 64:
                    j = i + 1
                    while j < nkb and keyblocks[j][1] == 64 and keyblocks[j][2] == bcol:
                        j += 1
                    bks = sorted(set(b0 + (c * N) // 512 for c in range(i, j)))
                    P.act(lambda: S.activation(out=pt_[:, i * N:j * N], in_=psf[:, i * N:j * N], func=AF.Exp, scale=0.125,
                                               bias=cv_[:, bcol:bcol + 1]), r=[bank[x] for x in bks] + [r_c], w=[rpt])
                    i = j
                else:
                    bk = b0 + (i * N) // 512
                    for p in range(2):
                        P.act(lambda: S.activation(out=pt_[64 * p:64 * p + M, i * N:i * N + N],
                                                   in_=psf[64 * p:64 * p + M, i * N:i * N + N], func=AF.Exp, scale=0.125,
                                                   bias=cv_[64 * p:64 * p + M, bcol:bcol + 1]),
                              r=[bank[bk], r_c], w=[rpt])
                    i += 1
            attn_pend.append(lambda: attn_phase_b(keyblocks, nq, out_fn, h, sumw, b0, N, pt_, rpt, dn_, rdn))
            while len(attn_pend) > 1:
                attn_pend.pop(0)()

        def attn_phase_b(keyblocks, nq, out_fn, h, sumw, b0, N, pt_, rpt, dn_, rdn):
            bo = b0 + 2
            nb = len(keyblocks)
            for i, (kfn, M, bcol, vap, rk, rv) in enumerate(keyblocks):
                P.pe(lambda vap=vap, i=i: T.matmul(ps[:, bo, 0:N], lhsT=vap, rhs=pt_[:, i * N:i * N + N],
                                                   start=(i == 0), stop=(i == nb - 1)), r=[rv, rpt], w=[bank[bo]], sig=(i == nb - 1))
            for i, (kfn, M, bcol, vap, rk, rv) in enumerate(keyblocks):
                P.pe(lambda i=i: T.matmul(ps[:, bo, 256:256 + N], lhsT=cb[:, sumw[i], :], rhs=pt_[:, i * N:i * N + N],
                                          start=(i == 0), stop=(i == nb - 1)), r=[r_c, rpt], w=[bank[bo]], sig=(i == nb - 1))
            P.dve(lambda: V.tensor_tensor(out=dn_[:, 0:N].rearrange("p (g q) -> p g q", g=4),
                                          in0=ps[:, bo, 256:256 + N].rearrange("p (g q) -> p g q", g=4),
                                          in1=esk[:, 4 * h:4 * h + 4].unsqueeze(2).to_broadcast([128, 4, nq]), op=ALU.add),
                  r=[bank[bo], r_c], w=[rdn])
            P.act(lambda: S.activation(out=dn_[:, 0:N], in_=dn_[:, 0:N], func=AF.Ln), r=[rdn], w=[rdn])
            P.act(lambda: S.activation(out=dn_[:, 0:N], in_=dn_[:, 0:N], func=AF.Exp, scale=-1.0), r=[rdn], w=[rdn])
            P.dve(lambda: V.tensor_tensor(out=out_fn(), in0=ps[:, bo, 0:N].rearrange("p (g q) -> p g q", g=4),
                                          in1=dn_[:, 0:N].rearrange("p (g q) -> p g q", g=4), op=ALU.mult),
                  r=[bank[bo], rdn], w=[r_ybT])

        n_main_tiles = nt_sg - first_main_li - (1 if has_samp else 0)
        for li in range(first_main_li, first_main_li + n_main_tiles):
            for c2 in range(2):
                tok0 = li * 128 + 64 * c2
                for h in range(2):
                    kbs = []
                    for i in range(3):
                        kcol = 128 + tok0 - 128 + 64 * i
                        chv = 2 + (tok0 // 64) - 2 + i
                        is_halo = (sgi == 0 and (tok0 - 128 + 64 * i) < 128)
                        bcol = C_MASK if is_halo else C_ZERO
                        kbs.append((lambda p, kcol=kcol, h=h: kTp[:, 2 * h + p, kcol:kcol + 64], 64, bcol,
                                    Vbd[:, chv, h, :], r_kTp, r_Vbd))
                    attn_block(kbs, lambda h=h, tok0=tok0: [(qT[:, tok0 // 64, 4 * h:4 * h + 4, :].rearrange("p a b -> p (a b)"), 256)], 64,
                               lambda h=h, tok0=tok0: ybT[:, 4 * h:4 * h + 4, tok0:tok0 + 64], h, [1, 1, 1])
        attn_flush()
        while pool_thunks:
            pool_thunks.pop(0)()
        c_lo2 = first_main_li * 128
        for g in range(4):
            for (g0, n) in groups_of(c_lo2, CT):
                k = cnt["gu"]
                cnt["gu"] += 1
                bk = k % 4
                P.pe(lambda g=g, bk=bk, g0=g0, n=n: T.matmul(ps[:, bk, 0:n], lhsT=Wpw[:, g, :], rhs=pooled[:, g, g0:g0 + n],
                                                             start=True, stop=True), r=[r_Wpw, r_pooled], w=[bank[bk]])
                P.act(lambda g=g, bk=bk, g0=g0, n=n: S.activation(out=yaT[:, g, g0:g0 + n], in_=ps[:, bk, 0:n],
                                                                  func=AF.Identity, scale=cv_[:, C_PSC + g:C_PSC + g + 1],
                                                                  bias=bsc[:, g:g + 1]), r=[bank[bk], r_c], w=[r_yaT])

        chk('attn')
        if has_samp:
            lis = nt_sg - 1
            r_samp2 = R("samp2")
            for s in range(2):
                for var in range(4):
                    k = cnt["gu"]
                    cnt["gu"] += 1
                    bk = k % 4
                    P.pe(lambda s=s, var=var, bk=bk: T.matmul(ps[:, bk, 0:128], lhsT=kcp[:, s, var, :], rhs=cb[:, 0, :],
                                                              start=True, stop=True), r=[r_samp2, r_c], w=[bank[bk]],
                         extra=(dks if (s == 0 and var == 0) else ()))
                    P.act(lambda s=s, var=var, bk=bk: S.copy(out=ksT[:, s, var, :], in_=ps[:, bk, 0:128]),
                          r=[bank[bk]], w=[r_samp2])
                k = cnt["gu"]
                cnt["gu"] += 1
                bk = k % 4
                tok0 = lis * 128 + 32 * s
                for p in range(2):
                    for kc in range(KC):
                        P.pe(lambda kc=kc, p=p, bk=bk, tok0=tok0: T.matmul(
                            ps[64 * p:64 * p + 32, bk, 0:128], lhsT=hT[:, kc, tok0:tok0 + 32], rhs=Wv[:, kc, :],
                            start=(kc == 0), stop=(kc == KC - 1)), r=[r_Wv] + rh(tok0, 32), w=[bank[bk]], sig=(kc == KC - 1))
                for p in range(2):
                    P.act(lambda p=p, bk=bk, s=s: S.copy(
                        out=Vs[64 * p:64 * p + 32, s, 2, :, 64 * p:64 * p + 64],
                        in_=ps[64 * p:64 * p + 32, bk, 0:128].rearrange("q (h d) -> q h d", h=2)),
                        r=[bank[bk]], w=[r_samp2])
            for s in range(2):
                tok0 = lis * 128 + 32 * s
                for h in range(2):
                    kbs = []
                    for i in range(2):
                        kbs.append((lambda p, s=s, h=h, i=i: ksT[:, s, 2 * h + p, 64 * i:64 * i + 64], 64, C_ZERO,
                                    Vs[:, s, i, h, :], r_samp2, r_samp2))
                    kbs.append((lambda p, h=h, tok0=tok0: kTp[:, 2 * h + p, 128 + tok0:128 + tok0 + 32], 32, C_ZERO,
                                Vs[:, s, 2, h, :], r_kTp, r_samp2))
                    attn_block(kbs, lambda h=h, tok0=tok0: [(qT[:, tok0 // 64, 4 * h + g2, tok0 % 64:tok0 % 64 + 32], 32) for g2 in range(4)], 32,
                               lambda h=h, tok0=tok0: ybT[:, 4 * h:4 * h + 4, tok0:tok0 + 32], h, [1, 1, 3])
            attn_flush()
            P.dve(lambda: V.memset(ybT[:, :, lis * 128 + 64:lis * 128 + 128], 0.0), w=[r_ybT])
        chk('sattn')
        for li, t in enumerate(tiles):
            if t in (LAST_MAIN, SAMP_TILE):
                P.pe(lambda t=t: T.transpose(out=ps[:, 7, 0:128], in_=kfull_t[t][:], identity=id32[:]),
                     r=[r_kf[t], r_c], w=[bank[7]])
                P.act(lambda: S.copy(out=osb2[:, 0:128], in_=ps[:, 7, 0:128]), r=[bank[7]], w=[r_osb2])
                if t == LAST_MAIN:
                    out_dmas.append(P.dma("sp", lambda: nc.sync.dma_start(out=o_k_p, in_=osb2[:, 0:128]), r=[r_osb2]))
                else:
                    for s in range(2):
                        out_dmas.append(P.dma("sp", lambda s=s: nc.sync.dma_start(
                            out=o_k_s[s, 96:128, :], in_=osb2[32 * s:32 * s + 32, 0:128]), r=[r_osb2]))

        chk('kout')
        c_lo3 = first_main_li * 128
        grps3 = groups_of(c_lo3, CT)
        gate_blocks = [get_block() for _ in range(0)]
        for half in range(2):
            buf, rb, extra = get_block()
            for c4 in range(4):
                c = 4 * half + c4
                for gi, (g0, n) in enumerate(grps3):
                    k = cnt["gu"]
                    cnt["gu"] += 1
                    bg, bp = 2 * (k % 2), 2 * (k % 2) + 1
                    for kc in range(KC):
                        P.pe(lambda: T.matmul(ps[:, bg, 0:n], lhsT=buf[:, kc, 128 * c4:128 * c4 + 128],
                                              rhs=hT[:, kc, g0:g0 + n], start=(kc == 0), stop=(kc == KC - 1)),
                             r=rb + rh(g0, n), w=[bank[bg]], sig=(kc == KC - 1))
                    for g in range(4):
                        P.pe(lambda: T.matmul(ps[:, bp, 0:n], lhsT=Wpp[:, g, 128 * c:128 * c + 128],
                                              rhs=yaT[:, g, g0:g0 + n], start=(g == 0), stop=(g == 3)),
                             r=[r_Wpp, r_yaT], w=[bank[bp]], sig=(g == 3))
                    gt_, rgt = gtmp[k % 2], r_gtmp[k % 2]
                    P.act(lambda: S.activation(out=gt_[:, 0:n], in_=ps[:, bg, 0:n], func=AF.Sigmoid,
                                               bias=cv_[:, C_BGA + c:C_BGA + c + 1]), r=[bank[bg], r_c], w=[rgt])
                    P.dve(lambda: V.tensor_tensor(out=tT[:, c4, g0:g0 + n], in0=ps[:, bp, 0:n], in1=gt_[:, 0:n],
                                                  op=ALU.mult), r=[bank[bp], rgt], w=[r_uT])
            buf, rb, extra = get_block()
            for c4 in range(4):
                c = 4 * half + c4
                for gi, (g0, n) in enumerate(grps3):
                    k = cnt["gu"]
                    cnt["gu"] += 1
                    bg, bp = 2 * (k % 2), 2 * (k % 2) + 1
                    for kc in range(KC):
                        P.pe(lambda: T.matmul(ps[:, bg, 0:n], lhsT=buf[:, kc, 128 * c4:128 * c4 + 128],
                                              rhs=hT[:, kc, g0:g0 + n], start=(kc == 0), stop=(kc == KC - 1)),
                             r=rb + rh(g0, n), w=[bank[bg]], sig=(kc == KC - 1))
                    for m in range(8):
                        P.pe(lambda: T.matmul(ps[:, bp, 0:n], lhsT=Wap[:, m, 128 * c:128 * c + 128],
                                              rhs=ybT[:, m, g0:g0 + n], start=(m == 0), stop=(m == 7)),
                             r=[r_Wap, r_ybT], w=[bank[bp]], sig=(m == 7))
                    gt_, rgt = gtmp[k % 2], r_gtmp[k % 2]
                    P.act(lambda: S.activation(out=gt_[:, 0:n], in_=ps[:, bg, 0:n], func=AF.Sigmoid,
                                               bias=cv_[:, C_BGB + c:C_BGB + c + 1]), r=[bank[bg], r_c], w=[rgt])
                    P.dve(lambda: V.tensor_tensor(out=gt_[:, 0:n], in0=ps[:, bp, 0:n], in1=gt_[:, 0:n], op=ALU.mult),
                          r=[bank[bp], rgt], w=[rgt])
                    P.dve(lambda: V.tensor_tensor(out=mT[:, c, g0:g0 + n], in0=tT[:, c4, g0:g0 + n], in1=gt_[:, 0:n],
                                                  op=ALU.add), r=[r_uT, rgt], w=[r_qT])
        if pre_out is not None:
            pre_out()
        for li in range(first_main_li, nt_sg):
            k = cnt["wo"]
            cnt["wo"] += 1
            b0 = 4 + 2 * (k % 2)
            for nh in range(2):
                for c in range(8):
                    P.pe(lambda c=c, nh=nh, li=li, b0=b0: T.matmul(ps[:, b0 + nh, :], lhsT=mT[:, c, li * 128:li * 128 + 128],
                                                                   rhs=Wom[:, c, nh * 512:nh * 512 + 512], start=(c == 0),
                                                                   stop=(c == 7)), r=[r_qT, r_Wom], w=[bank[b0 + nh]], sig=(c == 7))
            P.dve(lambda li=li, b0=b0: V.tensor_tensor(out=X[:, li, :], in0=ps[:, b0:b0 + 2, :].rearrange("p a b -> p (a b)"),
                                                       in1=X[:, li, :], op=ALU.add),
                  r=[bank[b0], bank[b0 + 1], rX[li]], w=[rX[li]])
            if hook is not None:
                hook(li)
        if sgi + 1 < len(SGS):
            P.act(lambda: S.copy(out=kT_hist[:], in_=kTp[:, :, CT:CT + 128]), r=[r_kTp], w=[r_hist])
            P.act(lambda: S.copy(out=V_hist[:], in_=Vbd[:, 2 * nt_sg:2 * nt_sg + 2, :, :]), r=[r_Vbd], w=[r_hist])
            P.act(lambda: S.copy(out=u_hist[:], in_=uT[:, :, CT:CT + 16]), r=[r_uT], w=[r_hist])

    kfull_t = {LAST_MAIN: kfull, SAMP_TILE: kfull2}
    r_kf = {LAST_MAIN: r_kfull, SAMP_TILE: R("kfull2")}

    try:
        for sgi, tiles in enumerate(SGS):
            tl = [(li, t) for li, t in enumerate(tiles)]
            tl2 = [(li, t) for li, t in enumerate(tiles) if t != 0]
            if sgi == 0:
                load_g(0)
                for li, t in enumerate(tiles):
                    load_x(sgi, li)
                norm_stage([(li, li * 128) for li, t in enumerate(tiles)])
            chk('norm1')
            ffn(sgi, 1, tl, store=False, pre_loop=lambda: (load_g(1), norm_begin(len(tl))), hook=lambda li: norm_push(li, li * 128, 0))
            chk('ffn1')
            mix(sgi, tiles, pre_out=lambda: (load_g(2), norm_begin(len(tl2))), hook=lambda li: norm_push(li, li * 128, 0))
            chk('mix')
            if sgi + 1 < len(SGS):
                nxt = SGS[sgi + 1]

                def pre2(sgi=sgi, nxt=nxt, tl2=tl2):
                    load_g(0)
                    norm_begin(len(nxt))
                    done_li = set(li for (li, _) in tl2)
                    for li in range(len(nxt)):
                        if li not in done_li:
                            norm_push(li, li * 128, 0)

                def hook2(li, nxt=nxt):
                    if li < len(nxt):
                        norm_push(li, li * 128, 0)
                ffn(sgi, 2, tl2, store=True, pre_loop=pre2, hook=hook2, guard=True, hook_lag=2)
            else:
                ffn(sgi, 2, tl2, store=True, guard=True)
    except _Stop:
        pass
    P.add("sp", lambda: nc.sync.nop(), extra=out_dmas, sig=False)
    P.emit()
    return nc


_NC_CACHE = {}


def make_core_inputs(inputs, nt_main, n_cores=8):
    f = lambda a: np.ascontiguousarray(np.asarray(a, dtype=np.float32))
    xp, xs = f(inputs["x_prompt"]), f(inputs["x_sample"])
    T_MAIN = nt_main * 128
    NT = nt_main + 2
    ident = np.eye(128, dtype=np.float32)
    onesbd = np.zeros((128, 128), np.float32)
    onesbd[:64, :64] = 1.0
    onesbd[64:, 64:] = 1.0
    onesbd32 = np.zeros((128, 128), np.float32)
    onesbd32[0:32, 0:64] = 1.0
    onesbd32[64:96, 64:128] = 1.0
    cst = np.concatenate([ident, onesbd, onesbd / 64.0, onesbd32], axis=1)
    g3 = np.stack([f(inputs["norm_ffn1"])[0], f(inputs["norm_mix"])[0], f(inputs["norm_ffn2"])[0]])
    g3 = np.ascontiguousarray(np.broadcast_to(g3[:, None, :], (3, 128, D)))
    qn, kn = f(inputs["q_norm"])[0], f(inputs["k_norm"])[0]
    bg = f(inputs["b_gate"])[0]
    psc, pb = f(inputs["pool_scale"])[0], f(inputs["pool_b"])[0]
    sinks = f(inputs["sinks"])[0]
    shared = dict(
        f1wi=f(inputs["ffn1_w_in"])[0], f1wo=f(inputs["ffn1_w_out"])[0], wmix=f(inputs["w_in"])[0],
        poolw=f(inputs["pool_w"])[0], wpp=f(inputs["w_pool_proj"])[0], wap=f(inputs["w_attn_proj"])[0],
        wom=f(inputs["w_out"])[0], f2wi=f(inputs["ffn2_w_in"])[0], f2wo=f(inputs["ffn2_w_out"])[0],
        g3=g3, cst=cst)
    maps = []
    pidx = np.arange(128)
    for c in range(n_cores):
        b, half = c // 2, c % 2
        xin = np.zeros((NT * 128, D), np.float32)
        if half == 1:
            xin[0:128] = xp[b, T_MAIN - 128:T_MAIN]
        xin[128:128 + T_MAIN] = xp[b, half * T_MAIN:(half + 1) * T_MAIN]
        xin[128 + T_MAIN:128 + T_MAIN + 32] = xs[2 * c]
        xin[128 + T_MAIN + 32:128 + T_MAIN + 64] = xs[2 * c + 1]
        colv = np.zeros((128, NCOL), np.float32)
        colv[:, C_GQ] = qn[pidx % 64]
        colv[:, C_GK] = kn[pidx % 64]
        for cc in range(8):
            colv[:, C_BGA + cc] = bg[0, cc * 128:(cc + 1) * 128]
            colv[:, C_BGB + cc] = bg[1, cc * 128:(cc + 1) * 128]
        for g in range(4):
            colv[:, C_PSC + g] = psc[g * 128:(g + 1) * 128]
            colv[:, C_PB + g] = pb[g]
        for h in range(2):
            for g2 in range(4):
                colv[0:64, C_SINK + 4 * h + g2] = sinks[h * 8 + 2 * g2]
                colv[64:128, C_SINK + 4 * h + g2] = sinks[h * 8 + 2 * g2 + 1]
        colv[:, C_MASK] = -30000.0 if half == 0 else 0.0
        colv[:, C_EPS] = EPS
        for g in range(4):
            w_ = 2 ** (g + 1)
            for j in range(16):
                colv[:, C_INV + 16 * g + j] = (float(w_) / min(j + 1, w_)) if half == 0 else 1.0
        m = dict(shared)
        m.update(xin=xin, colv=colv,
                 spool=f(inputs["state_pool"])[0, 2 * c:2 * c + 2],
                 ck=f(inputs["cache_k"])[0, 2 * c:2 * c + 2].reshape(2, 128, 128),
                 cv=f(inputs["cache_v"])[0, 2 * c:2 * c + 2].reshape(2, 128, 128))
        maps.append(m)
    return maps


def assemble(results, nt_main, n_cores=8):
    T_MAIN = nt_main * 128
    B = n_cores // 2
    S = 2 * T_MAIN
    yp = np.zeros((B, S, D), np.float32)
    ys = np.zeros((2 * n_cores, 32, D), np.float32)
    pp = np.zeros((1, B, 15, 512), np.float32)
    kp = np.zeros((1, B, 128, 2, 64), np.float32)
    vp = np.zeros((1, B, 128, 2, 64), np.float32)
    pss = np.zeros((1, 2 * n_cores, 15, 512), np.float32)
    ks = np.zeros((1, 2 * n_cores, 128, 2, 64), np.float32)
    vs = np.zeros((1, 2 * n_cores, 128, 2, 64), np.float32)
    for c, r in enumerate(results):
        b, half = c // 2, c % 2
        yp[b, half * T_MAIN:(half + 1) * T_MAIN] = r["y_main"]
        ys[2 * c] = r["y_samp"][0:32]
        ys[2 * c + 1] = r["y_samp"][32:64]
        if half == 1:
            pp[0, b] = r["o_pool_p"]
            kp[0, b] = r["o_k_p"].reshape(128, 2, 64)
            vp[0, b] = r["o_v_p"].reshape(128, 2, 64)
        pss[0, 2 * c:2 * c + 2] = r["o_pool_s"]
        ks[0, 2 * c:2 * c + 2] = r["o_k_s"].reshape(2, 128, 2, 64)
        vs[0, 2 * c:2 * c + 2] = r["o_v_s"].reshape(2, 128, 2, 64)
    return (yp, ys, pp, kp, vp, pss, ks, vs)


def kernel(**inputs):
    nt_main = 16
    if nt_main not in _NC_CACHE:
        _NC_CACHE[nt_main] = build(nt_main)
    nc = _NC_CACHE[nt_main]
    maps = make_core_inputs(inputs, nt_main)
    res = run_bass_kernel_spmd(nc, maps, core_ids=list(range(8)))
    return assemble(res.results, nt_main)
```

```python
import numpy as np
import concourse.bass as bass
import concourse.mybir as mybir
from concourse.bass_utils import run_bass_kernel_spmd

F32 = mybir.dt.float32
BF16 = mybir.dt.bfloat16
AF = mybir.ActivationFunctionType
ALU = mybir.AluOpType

D = 1024
KC = 8
DFF = 2816
NJ = 22
NJP = 11
DIN = 3840
EPS = 1e-6
NCOL = 128
C_GQ, C_GK, C_BGA, C_BGB, C_PSC, C_PB, C_SINK, C_MASK, C_ZERO, C_INV, C_EPS = 0, 1, 2, 10, 18, 22, 26, 34, 35, 40, 36


class Res:
    __slots__ = ("name", "w", "rd", "excl")

    def __init__(self, name, excl=False):
        self.name = name
        self.w = None
        self.rd = {}
        self.excl = excl


class Op:
    __slots__ = ("eng", "dma", "sig", "cnt", "sem", "semval")


class Prog:
    def __init__(self, nc, ndma=10):
        self.nc = nc
        self.h = {"pe": nc.tensor, "act": nc.scalar, "dve": nc.vector, "pool": nc.gpsimd, "sp": nc.sync}
        self.ndma = ndma
        self.last = {}
        self.csem = {e: nc.alloc_semaphore("c_" + e) for e in ("pe", "act", "dve", "pool")}
        self.ccount = {e: 0 for e in self.csem}
        self.dpool = {q: [[nc.alloc_semaphore("d_%s%d" % (q, i)), 0] for i in range(ndma)] for q in ("sp", "pool")}
        self.dnext = {"sp": 0, "pool": 0}
        self.seen = {e: {} for e in self.h}
        self.pending = {e: [] for e in self.h}
        self.nops = 0

    def defer_start(self, eng):
        self.defer_eng, self.deferred = eng, []

    def defer_stop(self):
        d = self.deferred
        self.defer_eng, self.deferred = None, None
        return d

    def add(self, eng, fn, reads=(), writes=(), dma=False, extra=(), sig=True):
        if getattr(self, "defer_eng", None) == eng and not dma:
            self.deferred.append(lambda: self.add(eng, fn, reads, writes, dma=dma, extra=extra, sig=sig))
            return None
        o = Op()
        o.eng, o.dma, o.sig, o.cnt, o.sem, o.semval = eng, dma, (sig or dma), 0, None, 0
        self.nops += 1
        deps = set(extra)
        for r in reads:
            if r.w is not None:
                deps.add(r.w)
            if r.excl:
                for rd in r.rd.values():
                    if rd.eng != eng:
                        deps.add(rd)
        for w in writes:
            if w.w is not None and (dma or w.w.dma or w.w.eng != eng or eng != "pe"):
                deps.add(w.w)
            for rd in w.rd.values():
                if rd is not o and (dma or rd.dma or rd.eng != eng or eng != "pe"):
                    deps.add(rd)
        E = self.h[eng]
        waits = {}
        for d in deps:
            if d.dma:
                s, v = d.sem, d.semval
            else:
                if not d.sig:
                    assert d.eng == eng, "cross-engine dependency on unsignalled op"
                    continue
                s, v = self.csem[d.eng], d.cnt
            assert v > 0
            if waits.get(s.num, (None, 0))[1] < v:
                waits[s.num] = (s, v)
        slot = None
        if dma:
            slot = self.dpool[eng][self.dnext[eng] % self.ndma]
            self.dnext[eng] += 1
            if slot[1] > 0 and waits.get(slot[0].num, (None, 0))[1] < slot[1]:
                waits[slot[0].num] = (slot[0], slot[1])
        sn = self.seen[eng]
        for k in sorted(waits):
            s, v = waits[k]
            if sn.get(k, 0) >= v:
                continue
            E.wait_ge(s, v)
            sn[k] = v
        inst = fn()
        if dma:
            slot[1] += 16
            inst.then_inc(slot[0], 16)
            o.sem, o.semval = slot[0], slot[1]
        elif o.sig:
            self.ccount[eng] += 1
            inst.then_inc(self.csem[eng], 1)
            o.cnt = self.ccount[eng]
        key = ("dma", self.nops) if dma else eng
        for r in reads:
            r.rd[key] = o
        for w in writes:
            w.w = o
            w.rd = {}
        if not dma:
            if o.sig:
                for (res, kind, po) in self.pending[eng]:
                    if kind == "r" and res.rd.get(eng) is po:
                        res.rd[eng] = o
                    if kind == "w" and res.w is po:
                        res.w = o
                self.pending[eng] = []
                self.last[eng] = o
            else:
                for r in reads:
                    self.pending[eng].append((r, "r", o))
                for w in writes:
                    self.pending[eng].append((w, "w", o))
        return o

    def pe(self, fn, r=(), w=(), **k):
        return self.add("pe", fn, r, w, **k)

    def act(self, fn, r=(), w=(), **k):
        return self.add("act", fn, r, w, **k)

    def dve(self, fn, r=(), w=(), **k):
        return self.add("dve", fn, r, w, **k)

    def gps(self, fn, r=(), w=(), **k):
        return self.add("pool", fn, r, w, **k)

    def dma(self, q, fn, r=(), w=(), **k):
        return self.add(q, fn, r, w, dma=True, **k)

    def barrier(self):
        for e in ("pe", "act", "dve"):
            assert not self.pending[e], "barrier with unsignalled ops pending on " + e
        lasts = [self.last[e] for e in ("pe", "act", "dve") if e in self.last]
        for e in ("pe", "act", "dve"):
            ex = [o for o in lasts if o.eng != e]
            self.add(e, (lambda E=self.h[e]: E.nop()), extra=ex, sig=False)
        return lasts

    def phase_guard(self):
        for e in ("pe", "act", "dve", "pool"):
            assert not self.pending[e], "phase guard with unsignalled ops pending on " + e
        return [self.last[e] for e in ("pe", "act", "dve", "pool") if e in self.last]

    def emit(self):
        pass


def sg_split(nt_main):
    nt = nt_main + 2
    nsg = (nt + 5) // 6
    base, rem = divmod(nt, nsg)
    out, t = [], 0
    for i in range(nsg):
        n = base + (1 if i < rem else 0)
        out.append(list(range(t, t + n)))
        t += n
    return out


def groups_of(c0, c1):
    g = []
    while c0 < c1:
        n = min(512, c1 - c0)
        g.append((c0, n))
        c0 += n
    return g


class _Stop(Exception):
    pass


def build(nt_main, stop=None):
    def chk(name):
        if stop == name:
            raise _Stop()

    nc = bass.Bass("TRN2", target_bir_lowering=False)
    NT = nt_main + 2
    SGS = sg_split(nt_main)
    T_MAIN = nt_main * 128
    SAMP_TILE = nt_main + 1
    LAST_MAIN = nt_main

    def din(name, shape):
        return nc.dram_tensor(name, list(shape), F32, kind="ExternalInput").ap()

    def dout(name, shape):
        return nc.dram_tensor(name, list(shape), F32, kind="ExternalOutput").ap()

    xin = din("xin", [NT * 128, D])
    spool = din("spool", [2, 15, 512])
    ck = din("ck", [2, 128, 128])
    cv = din("cv", [2, 128, 128])
    f1wi = din("f1wi", [D, 2 * DFF])
    f1wo = din("f1wo", [DFF, D])
    wmix = din("wmix", [D, DIN])
    poolw = din("poolw", [4, 128, 128])
    wpp = din("wpp", [512, D])
    wap = din("wap", [D, D])
    wom = din("wom", [D, D])
    f2wi = din("f2wi", [D, 2 * DFF])
    f2wo = din("f2wo", [DFF, D])
    g3 = din("g3", [3, 128, D])
    colv = din("colv", [128, NCOL])
    cst = din("cst", [128, 4 * 128])

    y_main = dout("y_main", [T_MAIN, D])
    y_samp = dout("y_samp", [64, D])
    o_pool_p = dout("o_pool_p", [15, 512])
    o_k_p = dout("o_k_p", [128, 128])
    o_v_p = dout("o_v_p", [128, 128])
    o_pool_s = dout("o_pool_s", [2, 15, 512])
    o_k_s = dout("o_k_s", [2, 128, 128])
    o_v_s = dout("o_v_s", [2, 128, 128])

    cur = [((nc.sbuf_base + 63) // 64) * 64]

    def alloc(name, shape, dt, at=None):
        esz = 4 if dt == F32 else 2
        n = esz
        for s in shape[1:]:
            n *= s
        n = ((n + 63) // 64) * 64
        if at is None:
            off = cur[0]
            cur[0] += n
        else:
            off = at
        assert off + n <= nc.sbuf_top, (name, off, n, nc.sbuf_top)
        return nc.alloc_sbuf_tensor_at(name, list(shape), dt, offset=off), off + n

    def A_(name, shape, dt):
        return alloc(name, shape, dt)[0]

    TS = 768
    X = A_("X", [128, 6, D], F32)
    wst = [A_("wst%d" % i, [128, KC, 512], BF16) for i in range(3)]
    Wpp = A_("Wpp", [128, 4, D], BF16)
    Wap = A_("Wap", [128, 8, D], BF16)
    Wom = A_("Wom", [128, 8, D], BF16)
    Wpw = A_("Wpw", [128, 4, 128], BF16)
    Wk = A_("Wk", [128, KC, 128], BF16)
    Wv = A_("Wv", [128, KC, 128], BF16)
    hT = A_("hT", [128, KC, TS], BF16)
    gt = A_("gt", [128, D], F32)
    htile_off = cur[0]
    htile = [A_("htile%d" % i, [128, D], BF16) for i in range(2)]
    stmp_off = cur[0]
    stmp = [A_("stmp%d" % i, [128, 512], BF16) for i in range(2)]
    htile.append(nc.alloc_sbuf_tensor_at("htile2", [128, D], BF16, offset=stmp_off))
    cb = A_("cb", [128, 4, 128], BF16)
    id32 = A_("id32", [128, 128], F32)
    cv_ = A_("colv", [128, NCOL], F32)
    ss = A_("ss", [128, 64], F32)
    rs = A_("rs", [128, 64], F32)
    bsc = A_("bsc", [128, 4], F32)
    esk = A_("esk", [128, 8], F32)
    osb = A_("osb", [128, 512], F32)
    osb2 = A_("osb2", [128, 512], F32)
    kfull = A_("kfull", [128, 128], F32)
    kfull2 = A_("kfull2", [128, 128], F32)
    fz = A_("fz", [128, 4], F32)
    kT_hist = A_("kT_hist", [128, 4, 128], BF16)
    V_hist = A_("V_hist", [128, 2, 2, 128], BF16)
    u_hist = A_("u_hist", [128, 4, 16], F32)
    phase0 = cur[0]
    Abuf, e1 = alloc("Abuf", [128, NJ, TS], BF16, at=phase0)
    Wo, e2 = alloc("Wo", [128, NJ, D], BF16, at=e1)
    mo = [phase0]

    def M_(name, shape, dt):
        t, e = alloc(name, shape, dt, at=mo[0])
        mo[0] = e
        return t

    qT_off = mo[0]
    qT = M_("qT", [128, TS // 64, 8, 64], BF16)
    uT = M_("uT", [128, 4, 16 + TS], F32)
    Sa = M_("Sa", [128, 16 + TS], F32)
    Sb = M_("Sb", [128, 16 + TS], F32)
    pooled_off = mo[0]
    pooled = M_("pooled", [128, 4, TS], BF16)
    yaT = M_("yaT", [128, 4, TS], BF16)
    ybT = M_("ybT", [128, 8, TS], BF16)
    kTp = M_("kTp", [128, 4, 128 + TS], BF16)
    Vbd = M_("Vbd", [128, 14, 2, 128], BF16)
    PT = [M_("PT%d" % i, [128, 768], BF16) for i in range(2)]
    den = [M_("den%d" % i, [128, 256], F32) for i in range(2)]
    sq = [M_("sq%d" % i, [128, 512], BF16) for i in range(2)]
    rstd_off = mo[0]
    rstd = [M_("rstd%d" % i, [128, 512], F32) for i in range(2)]
    gtmp_off = mo[0]
    gtmp = [M_("gtmp0", [128, 512], F32)] * 2
    kcp = nc.alloc_sbuf_tensor_at("kcp", [128, 2, 4, 128], BF16, offset=stmp_off)
    Vs = nc.alloc_sbuf_tensor_at("Vs", [128, 2, 3, 2, 128], BF16, offset=htile_off)
    ksT = nc.alloc_sbuf_tensor_at("ksT", [128, 2, 4, 128], BF16, offset=rstd_off + 2048)
    usT = M_("usT", [128, 4, 96], F32)
    sp16 = nc.alloc_sbuf_tensor_at("sp16", [16, 2, 512], F32, offset=rstd_off)

    mT = nc.alloc_sbuf_tensor_at("mT", [128, 8, TS], BF16, offset=qT_off)
    tT = uT

    ps = nc.alloc_psum_tensor("ps", [128, 8, 512], F32)
    psb = ps.bitcast(BF16)

    P = Prog(nc)
    R = Res
    bank = [R("bank%d" % i, excl=True) for i in range(8)]
    rX = [R("X%d" % i) for i in range(6)]
    r_wst = [[R("wst%dg" % i), R("wst%du" % i)] for i in range(3)]
    r_perm = R("permw")
    r_c = R("consts")
    r_gt = R("gt")
    r_hTt = [R("hT%d" % i) for i in range(6)]
    norm_q = []

    def rh(c0, n):
        while norm_q and norm_q[0][1] < c0 + n:
            _n_stage2(*norm_q.pop(0))
        return r_hTt[c0 // 128:(c0 + n - 1) // 128 + 1]
    r_htile = [R("ht0"), R("ht1"), R("ht2")]
    r_stmp = [R("st0"), R("st1")]
    r_ss = R("ss")
    r_ssc = [R("ss%d" % i) for i in range(64)]
    r_rsc = [R("rs%d" % i) for i in range(64)]
    r_rs = R("rs")
    r_A = R("A")
    r_Wo = [R("Wo%d" % i) for i in range(4)]
    r_osb = R("osb")
    r_osb2 = R("osb2")
    r_kfull = R("kfull")

    T, V, S, G = nc.tensor, nc.vector, nc.scalar, nc.gpsimd
    out_dmas = []

    P.dma("sp", lambda: nc.sync.dma_start(out=cv_[:], in_=colv), w=[r_c])
    P.dma("sp", lambda: nc.sync.dma_start(out=id32[:], in_=cst[:, 0:128]), w=[r_c])
    P.dma("pool", lambda: nc.gpsimd.dma_start(out=cb[:], in_=cst.rearrange("p (a b) -> p a b", a=4)), w=[r_c])
    P.dve(lambda: V.memset(ss[:], 0.0), w=[r_ss] + r_ssc)
    r_Wk = R("Wk")
    r_Wv, r_Wpw, r_Wpp, r_Wap, r_Wom = R("Wv"), R("Wpw"), R("Wpp"), R("Wap"), R("Wom")
    wq_k0 = 512 + 1024

    def load_perm():
        P.dma("pool", lambda: nc.gpsimd.dma_start(
            out=Wk[:], in_=wmix[:, wq_k0: wq_k0 + 128].rearrange("(kc q) n -> q kc n", q=128)), w=[r_Wk])
        P.dma("pool", lambda: nc.gpsimd.dma_start(
            out=Wv[:], in_=wmix[:, wq_k0 + 128: wq_k0 + 256].rearrange("(kc q) n -> q kc n", q=128)), w=[r_Wv])
        P.dma("pool", lambda: nc.gpsimd.dma_start(out=Wpw[:], in_=poolw.rearrange("g c d -> c g d")), w=[r_Wpw])
        P.dma("pool", lambda: nc.gpsimd.dma_start(out=Wpp[:], in_=wpp.rearrange("(g q) n -> q g n", q=128)), w=[r_Wpp])
        P.dma("pool", lambda: nc.gpsimd.dma_start(out=Wap[:], in_=wap.rearrange("(g q) n -> q g n", q=128)), w=[r_Wap])
        P.dma("pool", lambda: nc.gpsimd.dma_start(out=Wom[:], in_=wom.rearrange("(g q) n -> q g n", q=128)), w=[r_Wom])

    P.dve(lambda: V.tensor_tensor(out=bsc[:], in0=cv_[:, C_PB:C_PB + 4], in1=cv_[:, C_PSC:C_PSC + 4], op=ALU.mult),
          r=[r_c], w=[r_c])
    P.act(lambda: S.activation(out=esk[:], in_=cv_[:, C_SINK:C_SINK + 8], func=AF.Exp), r=[r_c], w=[r_c])
    for s in range(2):
        out_dmas.append(P.dma("sp", lambda s=s: nc.sync.dma_start(out=o_k_s[s, 0:96, :], in_=ck[s, 32:128, :])))
        out_dmas.append(P.dma("sp", lambda s=s: nc.sync.dma_start(out=o_v_s[s, 0:96, :], in_=cv[s, 32:128, :])))

    blocks = []
    for sg in range(len(SGS)):
        for jp in range(NJP):
            blocks.append(("f1", jp))
        for b in range(7):
            blocks.append(("mx", b))
        for jp in range(NJP):
            blocks.append(("f2", jp))
    st = {"issued": 0, "pos": 0}

    def issue_block(i):
        kind, a = blocks[i]
        buf = wst[i % 3]
        rb = r_wst[i % 3]
        if kind in ("f1", "f2"):
            wsrc = f1wi if kind == "f1" else f2wi
            v = wsrc.rearrange("(kc q) n -> q kc n", q=128)
            P.dma("pool", lambda: nc.gpsimd.dma_start(out=buf[:, :, 0:256], in_=v[:, :, 256 * a:256 * a + 256]), w=[rb[0]])
            P.dma("pool", lambda: nc.gpsimd.dma_start(out=buf[:, :, 256:512],
                                                      in_=v[:, :, DFF + 256 * a:DFF + 256 * a + 256]), w=[rb[1]])
        else:
            v = wmix.rearrange("(kc q) n -> q kc n", q=128)
            cols = [0, 512, 1024, 1792, 2816, 2304, 3328][a]
            P.dma("pool", lambda: nc.gpsimd.dma_start(out=buf[:], in_=v[:, :, cols:cols + 512]), w=rb)

    def get_block(look=3):
        i = st["pos"]
        while st["issued"] < min(i + look, len(blocks)):
            issue_block(st["issued"])
            st["issued"] += 1
        st["pos"] += 1
        return wst[i % 3], r_wst[i % 3], []

    cnt = {"norm": 0, "gu": 0, "wo": 0, "tr": 0, "pb": 0}

    def norm_stage(tl):
        norm_begin(len(tl))
        for (li, lc) in tl:
            norm_push(li, lc, 6)

    norm_seq = {"off": 0, "i": 0}

    def norm_begin(m):
        assert not norm_q
        norm_seq["off"] = (2 - m) % 3
        norm_seq["i"] = 0

    def _n_stage1(li):
        k = cnt["norm"]
        cnt["norm"] += 1
        hb = (norm_seq["i"] + norm_seq["off"]) % 3
        norm_seq["i"] += 1
        ht, rht = htile[hb], r_htile[hb]
        alias = r_stmp if hb == 2 else []
        col = k % 64
        P.act(lambda: S.activation(out=ht[:], in_=X[:, li, :], func=AF.Square, accum_out=ss[:, col:col + 1]),
              r=[rX[li], r_ssc[col]], w=[rht, r_ssc[col]] + alias)
        P.act(lambda: S.activation(out=rs[:, col:col + 1], in_=ss[:, col:col + 1], func=AF.Sqrt, scale=1.0 / D,
                                   bias=cv_[:, C_EPS:C_EPS + 1]), r=[r_ssc[col], r_c], w=[r_rsc[col]])
        P.dve(lambda: V.reciprocal(out=rs[:, col:col + 1], in_=rs[:, col:col + 1]), r=[r_rsc[col]], w=[r_rsc[col]])
        P.dve(lambda: V.scalar_tensor_tensor(out=ht[:], in0=X[:, li, :], scalar=rs[:, col:col + 1], in1=gt[:],
                                             op0=ALU.mult, op1=ALU.mult), r=[rX[li], r_rsc[col], r_gt], w=[rht] + alias)
        return (k, hb)

    def _n_stage2(kh, lc, tb0):
        k, hb = kh
        ht, rht = htile[hb], r_htile[hb]
        b = tb0 + (k % 2)
        for kc in range(KC):
            P.pe(lambda: T.transpose(out=psb[:, b, kc * 128:(kc + 1) * 128], in_=ht[:, kc * 128:(kc + 1) * 128],
                                     identity=cb[:, 0, :]), r=[rht, r_c], w=[bank[b]], sig=(kc == KC - 1))
        P.dve(lambda: V.tensor_copy(out=hT[:, :, lc:lc + 128], in_=psb[:, b, :].rearrange("p (k c) -> p k c", k=KC)),
              r=[bank[b]], w=[r_hTt[lc // 128]])

    def norm_push(li, lc, tb0):
        while len(norm_q) >= 2:
            _n_stage2(*norm_q.pop(0))
        k = _n_stage1(li)
        norm_q.append((k, lc, tb0))

    def norm_flush():
        while norm_q:
            _n_stage2(*norm_q.pop(0))

    def norm_flush_alias():
        idx = [i for i, e in enumerate(norm_q) if e[0][1] == 2]
        if idx:
            for _ in range(idx[-1] + 1):
                _n_stage2(*norm_q.pop(0))

    def load_g(i):
        P.dma("sp", lambda: nc.sync.dma_start(out=gt[:], in_=g3[i]), w=[r_gt])

    prefetched = set()

    def load_x(sg_i, li):
        t = SGS[sg_i][li]
        P.dma("sp", lambda: nc.sync.dma_start(out=X[:, li, :], in_=xin[t * 128:(t + 1) * 128, :]), w=[rX[li]])
        prefetched.add((sg_i, li))

    def ffn(sgi, which, tiles_li, store, pre_loop=None, hook=None, guard=False, hook_lag=0):
        wo_src = f1wo if which == 1 else f2wo
        if store and sgi + 1 < len(SGS):
            done_li = set(li for (li, _) in tiles_li)
            for li in range(len(SGS[sgi + 1])):
                if li not in done_li:
                    load_x(sgi + 1, li)
        norm_flush_alias()
        lasts = P.phase_guard() if guard else []
        st_guard = {"a": list(lasts)}
        c0 = tiles_li[0][0] * 128
        c1 = (tiles_li[-1][0] + 1) * 128
        if tiles_li[-1][1] == SAMP_TILE:
            c1 -= 64
            P.dve(lambda: V.memset(Abuf[:, :, c1:c1 + 64], 0.0), w=[r_A], extra=st_guard["a"])
            st_guard["a"] = []
        grps = groups_of(c0, c1)
        def fetch(jp, look=3):
            buf, rb, _ = get_block(look)
            if jp == 2 and sgi == 0 and which == 1:
                load_perm()
            if jp in (1, 3, 5, 7):
                qd = (jp - 1) // 2
                j0, j1 = [(0, 6), (6, 12), (12, 17), (17, 22)][qd]
                P.dma("pool", lambda: nc.gpsimd.dma_start(
                    out=Wo[:, j0:j1, :], in_=wo_src[j0 * 128:j1 * 128, :].rearrange("(j q) n -> q j n", q=128)),
                    w=[r_Wo[qd]], extra=lasts)
            return buf, rb

        def gate_up(jp, sub, buf, rb, g0, n):
            j = 2 * jp + sub
            k = cnt["gu"]
            cnt["gu"] += 1
            bg, bu = 2 * (k % 2), 2 * (k % 2) + 1
            for half, bk in ((0, bg), (1, bu)):
                for kc in range(KC):
                    P.pe(lambda: T.matmul(ps[:, bk, 0:n], lhsT=buf[:, kc, 256 * half + 128 * sub:256 * half + 128 * sub + 128],
                                          rhs=hT[:, kc, g0:g0 + n], start=(kc == 0), stop=(kc == KC - 1)),
                         r=[rb[half]] + rh(g0, n), w=[bank[bk]], sig=(kc == KC - 1))
            sb_, rsb = stmp[k % 2], r_stmp[k % 2]
            P.act(lambda: S.activation(out=sb_[:, 0:n], in_=ps[:, bg, 0:n], func=AF.Silu), r=[bank[bg]], w=[rsb])
            P.dve(lambda: V.tensor_tensor(out=Abuf[:, j, g0:g0 + n], in0=ps[:, bu, 0:n], in1=sb_[:, 0:n], op=ALU.mult),
                  r=[bank[bu], rsb], w=[r_A], extra=st_guard["a"])
            st_guard["a"] = []

        blk0 = fetch(0)
        blk1 = fetch(1, look=2)
        for gi, (g0, n) in enumerate(grps):
            for jp, (buf, rb) in ((0, blk0), (1, blk1)):
                for sub in range(2):
                    gate_up(jp, sub, buf, rb, g0, n)
        norm_flush()
        for jp in range(2, NJP):
            buf, rb = fetch(jp)
            for sub in range(2):
                for (g0, n) in grps:
                    gate_up(jp, sub, buf, rb, g0, n)
        if pre_loop is not None:
            pre_loop()
        hookq = []
        for (li, gt_) in tiles_li:
            k = cnt["wo"]
            cnt["wo"] += 1
            b0 = 4 + 2 * (k % 2)
            for nh in range(2):
                for j in range(NJ):
                    qd = 0 if j < 6 else 1 if j < 12 else 2 if j < 17 else 3
                    P.pe(lambda j=j, nh=nh: T.matmul(ps[:, b0 + nh, :], lhsT=Abuf[:, j, li * 128:li * 128 + 128],
                                                     rhs=Wo[:, j, nh * 512:nh * 512 + 512], start=(j == 0),
                                                     stop=(j == NJ - 1)), r=[r_A, r_Wo[qd]], w=[bank[b0 + nh]], sig=(j == NJ - 1))
            P.dve(lambda: V.scalar_tensor_tensor(out=X[:, li, :], in0=ps[:, b0:b0 + 2, :].rearrange("p a b -> p (a b)"),
                                                 scalar=0.5, in1=X[:, li, :], op0=ALU.mult, op1=ALU.add),
                  r=[bank[b0], bank[b0 + 1], rX[li]], w=[rX[li]])
            if store:
                if gt_ == SAMP_TILE:
                    out_dmas.append(P.dma("sp", lambda: nc.sync.dma_start(out=y_samp, in_=X[0:64, li, :]), r=[rX[li]]))
                else:
                    out_dmas.append(P.dma("sp", lambda gt_=gt_: nc.sync.dma_start(
                        out=y_main[(gt_ - 1) * 128:gt_ * 128, :], in_=X[:, li, :]), r=[rX[li]]))
                if sgi + 1 < len(SGS) and li < len(SGS[sgi + 1]):
                    load_x(sgi + 1, li)
            if hook is not None:
                hookq.append(li)
                while len(hookq) > hook_lag:
                    hook(hookq.pop(0))

        while hook is not None and hookq:
            hook(hookq.pop(0))

    r_qT, r_uT, r_Sa, r_Sb, r_pooled, r_yaT, r_ybT = R("qT"), R("uT"), R("Sa"), R("Sb"), R("pooled"), R("yaT"), R("ybT")
    r_kTp, r_Vbd = R("kTp"), R("Vbd")
    r_PT, r_den, r_sq, r_rstd, r_gtmp = [R("PT0"), R("PT1")], [R("den0"), R("den1")], [R("sq0"), R("sq1")], \
        [R("rstd0"), R("rstd1")], [R("gt0")] * 2
    r_samp = R("samp")
    r_hist = R("hist")
    r_usT = R("usT")

    qk_pend = []

    def qk_flush(keep=0):
        while len(qk_pend) > keep:
            qk_pend.pop(0)()

    def qk_norm(bk, n, gcol, emit_out):
        k = cnt["pb"]
        cnt["pb"] += 1
        sq_, rsq = sq[k % 2], r_sq[k % 2]
        rd_, rrd = rstd[k % 2], r_rstd[k % 2]
        mb = 4 + (k % 2)
        P.act(lambda: S.activation(out=sq_[:, 0:n], in_=ps[:, bk, 0:n], func=AF.Square), r=[bank[bk]], w=[rsq])

        def part_b():
            P.pe(lambda: T.matmul(ps[:, mb, 0:n], lhsT=cb[:, 2, :], rhs=sq_[:, 0:n], start=True, stop=True),
                 r=[rsq, r_c], w=[bank[mb]])
            P.act(lambda: S.activation(out=rd_[:, 0:n], in_=ps[:, mb, 0:n], func=AF.Ln, bias=cv_[:, C_EPS:C_EPS + 1]),
                  r=[bank[mb], r_c], w=[rrd])
            P.act(lambda: S.activation(out=rd_[:, 0:n], in_=rd_[:, 0:n], func=AF.Exp, scale=-0.5), r=[rrd], w=[rrd])
            emit_out(rd_, rrd)
        qk_flush(0)
        qk_pend.append(part_b)

    def mix(sgi, tiles, pre_out=None, hook=None):
        nt_sg = len(tiles)
        first_main_li = 1 if tiles[0] == 0 else 0
        has_samp = tiles[-1] == SAMP_TILE
        lasts = P.phase_guard()
        P.dve(lambda: V.memset(fz[:, 0:1], 0.0), extra=lasts, sig=False)
        P.act(lambda: S.copy(out=fz[:, 1:2], in_=cv_[:, C_ZERO:C_ZERO + 1]), extra=lasts, sig=False)
        P.gps(lambda: G.memset(fz[:, 2:3], 0.0), extra=lasts, sig=False)
        if sgi == 0:
            P.gps(lambda: G.memset(Vbd[:], 0.0), w=[r_Vbd])
            P.gps(lambda: G.memset(PT[0][:], 0.0), w=[r_PT[0]])
            P.gps(lambda: G.memset(PT[1][:], 0.0), w=[r_PT[1]])
            P.dve(lambda: V.memset(uT[:, :, 0:16], 0.0), w=[r_uT])
            P.dve(lambda: V.memset(kTp[:, :, 0:128], 0.0), w=[r_kTp])
        else:
            P.gps(lambda: G.memset(Vbd[:], 0.0), w=[r_Vbd])
            P.gps(lambda: G.memset(PT[0][:], 0.0), w=[r_PT[0]])
            P.gps(lambda: G.memset(PT[1][:], 0.0), w=[r_PT[1]])
            P.act(lambda: S.copy(out=kTp[:, :, 0:128], in_=kT_hist[:]), r=[r_hist], w=[r_kTp])
            P.act(lambda: S.copy(out=Vbd[:, 0:2, :, :], in_=V_hist[:]), r=[r_hist], w=[r_Vbd])
            P.act(lambda: S.copy(out=uT[:, :, 0:16], in_=u_hist[:]), r=[r_hist], w=[r_uT])
        dks = []
        if has_samp:
            norm_flush()
            P.dve(lambda: V.memset(kcp[:], 0.0), w=[r_samp] + r_htile + r_stmp)
            P.dve(lambda: V.memset(Vs[:], 0.0), w=[r_samp] + r_htile + r_stmp)
            dks = []
            for s in range(2):
                for h in range(2):
                    for p in range(2):
                        dks.append(P.dma("pool", lambda s=s, h=h, p=p: nc.gpsimd.dma_start(
                            out=kcp[:, s, 2 * h + p, 64 * p:64 * p + 64], in_=ck[s, :, 64 * h:64 * h + 64]),
                            r=[], w=[], extra=[r_samp.w]))
                for blk in range(2):
                    for p in range(2):
                        dks.append(P.dma("pool", lambda s=s, blk=blk, p=p: nc.gpsimd.dma_start(
                            out=Vs[64 * p:64 * p + 64, s, blk, :, 64 * p:64 * p + 64],
                            in_=cv[s, 64 * blk:64 * blk + 64, :].rearrange("k (h d) -> k h d", h=2)),
                            r=[], w=[], extra=[r_samp.w]))
        chk('mnorm')
        CT = nt_sg * 128
        grps_all = groups_of(0, CT)
        def v_proj(li_lo, li_hi):
            for li, t in list(enumerate(tiles))[li_lo:li_hi]:
                k = cnt["gu"]
                cnt["gu"] += 1
                bk = k % 4
                for kc in range(KC):
                    P.pe(lambda: T.matmul(ps[:, bk, 0:128], lhsT=hT[:, kc, li * 128:li * 128 + 128], rhs=Wv[:, kc, :],
                                          start=(kc == 0), stop=(kc == KC - 1)),
                         r=[r_Wv] + rh(li * 128, 128), w=[bank[bk]], sig=(kc == KC - 1))
                for c2 in range(2):
                    ch = 2 + 2 * li + c2
                    for p in range(2):
                        P.act(lambda: S.copy(
                            out=Vbd[64 * p:64 * p + 64, ch, :, 64 * p:64 * p + 64],
                            in_=ps[64 * c2:64 * c2 + 64, bk, 0:128].rearrange("q (h d) -> q h d", h=2)),
                            r=[bank[bk]], w=[r_Vbd])
                if t == LAST_MAIN:
                    P.dve(lambda: V.tensor_copy(out=osb[:, 0:128], in_=ps[:, bk, 0:128]), r=[bank[bk]], w=[r_osb])
                    out_dmas.append(P.dma("sp", lambda: nc.sync.dma_start(out=o_v_p, in_=osb[:, 0:128]), r=[r_osb]))
                if t == SAMP_TILE:
                    P.dve(lambda: V.tensor_copy(out=osb[0:64, 256:384], in_=ps[0:64, bk, 0:128]), r=[bank[bk]], w=[r_osb])
                    for s in range(2):
                        out_dmas.append(P.dma("sp", lambda s=s: nc.sync.dma_start(
                            out=o_v_s[s, 96:128, :], in_=osb[32 * s:32 * s + 32, 256:384]), r=[r_osb]))

        P.dve(lambda: V.memset(kTp[:, :, 128:128 + CT], 0.0), w=[r_kTp])
        for gi_k, (g0, n) in enumerate(grps_all):
            bk = 6 + (gi_k % 2)
            for kc in range(KC):
                P.pe(lambda: T.matmul(ps[:, bk, 0:n], lhsT=Wk[:, kc, :], rhs=hT[:, kc, g0:g0 + n], start=(kc == 0),
                                      stop=(kc == KC - 1)), r=[r_Wk] + rh(g0, n), w=[bank[bk]], sig=(kc == KC - 1))

            def emit_k(rd_, rrd, bk=bk, g0=g0, n=n):
                for h in range(2):
                    for p in range(2):
                        P.dve(lambda: V.scalar_tensor_tensor(
                            out=kTp[64 * p:64 * p + 64, 2 * h + p, 128 + g0:128 + g0 + n],
                            in0=ps[64 * h:64 * h + 64, bk, 0:n], scalar=cv_[64 * h:64 * h + 64, C_GK:C_GK + 1],
                            in1=rd_[64 * h:64 * h + 64, 0:n], op0=ALU.mult, op1=ALU.mult),
                            r=[bank[bk], rrd, r_c], w=[r_kTp])
                    for li, t in enumerate(tiles):
                        if t in (LAST_MAIN, SAMP_TILE) and g0 <= li * 128 < g0 + n:
                            o = li * 128 - g0
                            P.dve(lambda: V.scalar_tensor_tensor(
                                out=kfull_t[t][64 * h:64 * h + 64, :],
                                in0=ps[64 * h:64 * h + 64, bk, o:o + 128], scalar=cv_[64 * h:64 * h + 64, C_GK:C_GK + 1],
                                in1=rd_[64 * h:64 * h + 64, o:o + 128], op0=ALU.mult, op1=ALU.mult),
                                r=[bank[bk], rrd, r_c], w=[r_kf[t]])
            qk_norm(bk, n, C_GK, emit_k)
            v_proj(g0 // 128, (g0 + n) // 128)
        qk_flush(0)
        chk('kproj')
        if has_samp:
            P.dma("pool", lambda: nc.gpsimd.dma_start(out=sp16[0:15, :, :], in_=spool.rearrange("s r f -> r s f")),
                  w=[r_samp] + r_rstd, extra=lasts)
        chk('kv')
        buf, rb, extra = get_block()
        for g in range(4):
            for (g0, n) in grps_all:
                k = cnt["gu"]
                cnt["gu"] += 1
                bk = k % 4
                for kc in range(KC):
                    P.pe(lambda kc=kc, g=g, bk=bk: T.matmul(ps[:, bk, 0:n], lhsT=buf[:, kc, 128 * g:128 * g + 128],
                                                            rhs=hT[:, kc, g0:g0 + n], start=(kc == 0), stop=(kc == KC - 1)),
                         r=rb + rh(g0, n), w=[bank[bk]], extra=extra, sig=(kc == KC - 1))
                P.act(lambda g=g, bk=bk: S.copy(out=uT[:, g, 16 + g0:16 + g0 + n], in_=ps[:, bk, 0:n]),
                      r=[bank[bk]], w=[r_uT])
        def pool_group(U, g, L, rU):
            P.gps(lambda: G.tensor_tensor(out=Sa[:, 1:L], in0=U[:, g, 1:L], in1=U[:, g, 0:L - 1], op=ALU.add),
                  r=[rU], w=[r_Sa])
            if g >= 1:
                P.gps(lambda: G.tensor_tensor(out=Sb[:, 3:L], in0=Sa[:, 3:L], in1=Sa[:, 1:L - 2], op=ALU.add),
                      r=[r_Sa], w=[r_Sb])
            if g >= 2:
                P.gps(lambda: G.tensor_tensor(out=Sa[:, 7:L], in0=Sb[:, 7:L], in1=Sb[:, 3:L - 4], op=ALU.add),
                      r=[r_Sb], w=[r_Sa])
            if g >= 3:
                P.gps(lambda: G.tensor_tensor(out=Sb[:, 15:L], in0=Sa[:, 15:L], in1=Sa[:, 7:L - 8], op=ALU.add),
                      r=[r_Sa], w=[r_Sb])
            return (Sa, r_Sa) if g in (0, 2) else (Sb, r_Sb)

        n_pool_tiles = nt_sg - (1 if has_samp else 0)
        L = 16 + n_pool_tiles * 128
        c_lo = first_main_li * 128
        c_hi = n_pool_tiles * 128
        def pool_main_group(g):
            w_ = float(2 ** (g + 1))
            Sg_, rSg_ = pool_group(uT, g, L, r_uT)
            P.gps(lambda: G.tensor_scalar(out=Sg_[:, 16 + c_lo:16 + c_hi], in0=Sg_[:, 16 + c_lo:16 + c_hi], scalar1=1.0 / w_,
                                          scalar2=None, op0=ALU.mult), r=[rSg_], w=[rSg_])
            if sgi == 0:
                P.gps(lambda: G.tensor_tensor(out=Sg_[:, 16 + 128:16 + 144], in0=Sg_[:, 16 + 128:16 + 144],
                                              in1=cv_[:, C_INV + 16 * g:C_INV + 16 * g + 16], op=ALU.mult),
                      r=[rSg_, r_c], w=[rSg_])
            P.gps(lambda: G.tensor_tensor(out=pooled[:, g, c_lo:c_hi], in0=Sg_[:, 16 + c_lo:16 + c_hi],
                                          in1=uT[:, g, 16 + c_lo:16 + c_hi], op=ALU.subtract),
                  r=[rSg_, r_uT], w=[r_pooled])

        P.defer_start("pool")
        for g in range(4):
            pool_main_group(g)
        pool_thunks = P.defer_stop()
        qblk0 = get_block()
        if has_samp:
            lis = nt_sg - 1
            P.dve(lambda: V.memset(usT[:], 0.0), w=[r_usT])
            for s in range(2):
                for g in range(4):
                    P.pe(lambda s=s, g=g: T.transpose(out=ps[:, 7, 16 * (4 * s + g):16 * (4 * s + g) + 15],
                                                      in_=sp16[0:15, s, 128 * g:128 * g + 128], identity=id32[0:15, 0:15]),
                         r=[r_samp, r_c] + r_rstd, w=[bank[7]], sig=(s == 1 and g == 3))
            for s in range(2):
                P.act(lambda s=s: S.copy(out=usT[:, :, 48 * s + 1:48 * s + 16],
                                         in_=ps[:, 7, 64 * s:64 * s + 64].rearrange("p (g c) -> p g c", g=4)[:, :, 0:15]),
                      r=[bank[7]], w=[r_usT])
                P.act(lambda s=s: S.copy(out=usT[:, :, 48 * s + 16:48 * s + 48],
                                         in_=uT[:, :, 16 + lis * 128 + 32 * s:16 + lis * 128 + 32 * s + 32]),
                      r=[r_uT], w=[r_usT])
            for g in range(4):
                w_ = float(2 ** (g + 1))
                Sg_, rSg_ = pool_group(usT, g, 96, r_usT)
                for s in range(2):
                    P.gps(lambda: G.tensor_scalar(out=Sg_[:, 48 * s + 16:48 * s + 48], in0=Sg_[:, 48 * s + 16:48 * s + 48],
                                                  scalar1=1.0 / w_, scalar2=None, op0=ALU.mult), r=[rSg_], w=[rSg_])
                    P.gps(lambda: G.tensor_tensor(out=pooled[:, g, lis * 128 + 32 * s:lis * 128 + 32 * s + 32],
                                                  in0=Sg_[:, 48 * s + 16:48 * s + 48],
                                                  in1=usT[:, g, 48 * s + 16:48 * s + 48], op=ALU.subtract),
                          r=[rSg_, r_usT], w=[r_pooled])
            P.dve(lambda: V.memset(pooled[:, :, lis * 128 + 64:lis * 128 + 128], 0.0), w=[r_pooled])
            for g in range(4):
                P.pe(lambda g=g: T.transpose(out=ps[:, 7, 128 * g:128 * g + 128],
                                             in_=uT[:, g, 16 + lis * 128:16 + lis * 128 + 128], identity=id32[:]),
                     r=[r_uT, r_c], w=[bank[7]], sig=(g == 3))
            P.act(lambda: S.copy(out=osb2[:], in_=ps[:, 7, :]), r=[bank[7]], w=[r_osb2])
            for s in range(2):
                out_dmas.append(P.dma("sp", lambda s=s: nc.sync.dma_start(out=o_pool_s[s], in_=osb2[32 * s + 17:32 * s + 32, :]),
                                      r=[r_osb2]))
        for qb in range(2):
            buf, rb, extra = qblk0 if qb == 0 else get_block()
            for c4 in range(4):
                m = 4 * qb + c4
                for (g0, n) in groups_of(first_main_li * 128, CT):
                    k = cnt["gu"]
                    cnt["gu"] += 1
                    bk = k % 4
                    for kc in range(KC):
                        P.pe(lambda kc=kc, c4=c4, bk=bk: T.matmul(ps[:, bk, 0:n], lhsT=buf[:, kc, 128 * c4:128 * c4 + 128],
                                                                  rhs=hT[:, kc, g0:g0 + n], start=(kc == 0),
                                                                  stop=(kc == KC - 1)),
                             r=rb + rh(g0, n), w=[bank[bk]], extra=extra, sig=(kc == KC - 1))

                    def emit_q(rd_, rrd, bk=bk, m=m, g0=g0, n=n):
                        P.dve(lambda: V.scalar_tensor_tensor(
                            out=qT[:, g0 // 64:(g0 + n) // 64, m, :],
                            in0=ps[:, bk, 0:n].rearrange("p (c q) -> p c q", q=64),
                            scalar=cv_[:, C_GQ:C_GQ + 1], in1=rd_[:, 0:n].rearrange("p (c q) -> p c q", q=64),
                            op0=ALU.mult, op1=ALU.mult), r=[bank[bk], rrd, r_c], w=[r_qT])
                    qk_norm(bk, n, C_GQ, emit_q)
                    if pool_thunks:
                        pool_thunks.pop(0)()

        qk_flush(0)
        chk('proj')
        for li, t in enumerate(tiles):
            if t == LAST_MAIN:
                for g in range(4):
                    P.pe(lambda g=g, li=li: T.transpose(out=ps[:, 7, 128 * g:128 * g + 128],
                                                        in_=uT[:, g, 16 + li * 128:16 + li * 128 + 128], identity=id32[:]),
                         r=[r_uT, r_c], w=[bank[7]], sig=(g == 3))
                P.act(lambda: S.copy(out=osb2[:], in_=ps[:, 7, :]), r=[bank[7]], w=[r_osb2])
                out_dmas.append(P.dma("sp", lambda: nc.sync.dma_start(out=o_pool_p, in_=osb2[113:128, :]), r=[r_osb2]))
        chk('pool')
        attn_pend = []

        def attn_flush():
            while attn_pend:
                attn_pend.pop(0)()

        def attn_block(keyblocks, q_ap_fn, nq, out_fn, h, sumw):
            k = cnt["wo"]
            cnt["wo"] += 1
            b0 = 3 * (k % 2)
            N = 4 * nq
            pt_, rpt = PT[k % 2], r_PT[k % 2]
            dn_, rdn = den[k % 2], r_den[k % 2]
            for i, (kfn, M, bcol, vap, rk, rv) in enumerate(keyblocks):
                bk = b0 + (i * N) // 512
                off = (i * N) % 512
                for p in range(2):
                    qo = 0
                    for (qap, qn) in q_ap_fn():
                        P.pe(lambda: T.matmul(ps[64 * p:64 * p + M, bk, off + qo:off + qo + qn], lhsT=kfn(p), rhs=qap,
                                              start=True, stop=True), r=[rk, r_qT], w=[bank[bk]])
                        qo += qn
            psf = ps[:, b0:b0 + 2, :].rearrange("p a b -> p (a b)")
            i = 0
            nkb = len(keyblocks)
            while i < nkb:
                M, bcol = keyblocks[i][1], keyblocks[i][2]
                if M == 64:
                    j = i + 1
                    while j < nkb and keyblocks[j][1] == 64 and keyblocks[j][2] == bcol:
                        j += 1
                    bks = sorted(set(b0 + (c * N) // 512 for c in range(i, j)))
                    P.act(lambda: S.activation(out=pt_[:, i * N:j * N], in_=psf[:, i * N:j * N], func=AF.Exp, scale=0.125,
                                               bias=cv_[:, bcol:bcol + 1]), r=[bank[x] for x in bks] + [r_c], w=[rpt])
                    i = j
                else:
                    bk = b0 + (i * N) // 512
                    for p in range(2):
                        P.act(lambda: S.activation(out=pt_[64 * p:64 * p + M, i * N:i * N + N],
                                                   in_=psf[64 * p:64 * p + M, i * N:i * N + N], func=AF.Exp, scale=0.125,
                                                   bias=cv_[64 * p:64 * p + M, bcol:bcol + 1]),
                              r=[bank[bk], r_c], w=[rpt])
                    i += 1
            attn_pend.append(lambda: attn_phase_b(keyblocks, nq, out_fn, h, sumw, b0, N, pt_, rpt, dn_, rdn))
            while len(attn_pend) > 1:
                attn_pend.pop(0)()

        def attn_phase_b(keyblocks, nq, out_fn, h, sumw, b0, N, pt_, rpt, dn_, rdn):
            bo = b0 + 2
            nb = len(keyblocks)
            for i, (kfn, M, bcol, vap, rk, rv) in enumerate(keyblocks):
                P.pe(lambda vap=vap, i=i: T.matmul(ps[:, bo, 0:N], lhsT=vap, rhs=pt_[:, i * N:i * N + N],
                                                   start=(i == 0), stop=(i == nb - 1)), r=[rv, rpt], w=[bank[bo]], sig=(i == nb - 1))
            for i, (kfn, M, bcol, vap, rk, rv) in enumerate(keyblocks):
                P.pe(lambda i=i: T.matmul(ps[:, bo, 256:256 + N], lhsT=cb[:, sumw[i], :], rhs=pt_[:, i * N:i * N + N],
                                          start=(i == 0), stop=(i == nb - 1)), r=[r_c, rpt], w=[bank[bo]], sig=(i == nb - 1))
            P.dve(lambda: V.tensor_tensor(out=dn_[:, 0:N].rearrange("p (g q) -> p g q", g=4),
                                          in0=ps[:, bo, 256:256 + N].rearrange("p (g q) -> p g q", g=4),
                                          in1=esk[:, 4 * h:4 * h + 4].unsqueeze(2).to_broadcast([128, 4, nq]), op=ALU.add),
                  r=[bank[bo], r_c], w=[rdn])
            P.act(lambda: S.activation(out=dn_[:, 0:N], in_=dn_[:, 0:N], func=AF.Ln), r=[rdn], w=[rdn])
            P.act(lambda: S.activation(out=dn_[:, 0:N], in_=dn_[:, 0:N], func=AF.Exp, scale=-1.0), r=[rdn], w=[rdn])
            P.dve(lambda: V.tensor_tensor(out=out_fn(), in0=ps[:, bo, 0:N].rearrange("p (g q) -> p g q", g=4),
                                          in1=dn_[:, 0:N].rearrange("p (g q) -> p g q", g=4), op=ALU.mult),
                  r=[bank[bo], rdn], w=[r_ybT])

        n_main_tiles = nt_sg - first_main_li - (1 if has_samp else 0)
        for li in range(first_main_li, first_main_li + n_main_tiles):
            for c2 in range(2):
                tok0 = li * 128 + 64 * c2
                for h in range(2):
                    kbs = []
                    for i in range(3):
                        kcol = 128 + tok0 - 128 + 64 * i
                        chv = 2 + (tok0 // 64) - 2 + i
                        is_halo = (sgi == 0 and (tok0 - 128 + 64 * i) < 128)
                        bcol = C_MASK if is_halo else C_ZERO
                        kbs.append((lambda p, kcol=kcol, h=h: kTp[:, 2 * h + p, kcol:kcol + 64], 64, bcol,
                                    Vbd[:, chv, h, :], r_kTp, r_Vbd))
                    attn_block(kbs, lambda h=h, tok0=tok0: [(qT[:, tok0 // 64, 4 * h:4 * h + 4, :].rearrange("p a b -> p (a b)"), 256)], 64,
                               lambda h=h, tok0=tok0: ybT[:, 4 * h:4 * h + 4, tok0:tok0 + 64], h, [1, 1, 1])
        attn_flush()
        while pool_thunks:
            pool_thunks.pop(0)()
        c_lo2 = first_main_li * 128
        for g in range(4):
            for (g0, n) in groups_of(c_lo2, CT):
                k = cnt["gu"]
                cnt["gu"] += 1
                bk = k % 4
                P.pe(lambda g=g, bk=bk, g0=g0, n=n: T.matmul(ps[:, bk, 0:n], lhsT=Wpw[:, g, :], rhs=pooled[:, g, g0:g0 + n],
                                                             start=True, stop=True), r=[r_Wpw, r_pooled], w=[bank[bk]])
                P.act(lambda g=g, bk=bk, g0=g0, n=n: S.activation(out=yaT[:, g, g0:g0 + n], in_=ps[:, bk, 0:n],
                                                                  func=AF.Identity, scale=cv_[:, C_PSC + g:C_PSC + g + 1],
                                                                  bias=bsc[:, g:g + 1]), r=[bank[bk], r_c], w=[r_yaT])

        chk('attn')
        if has_samp:
            lis = nt_sg - 1
            r_samp2 = R("samp2")
            for s in range(2):
                for var in range(4):
                    k = cnt["gu"]
                    cnt["gu"] += 1
                    bk = k % 4
                    P.pe(lambda s=s, var=var, bk=bk: T.matmul(ps[:, bk, 0:128], lhsT=kcp[:, s, var, :], rhs=cb[:, 0, :],
                                                              start=True, stop=True), r=[r_samp2, r_c], w=[bank[bk]],
                         extra=(dks if (s == 0 and var == 0) else ()))
                    P.act(lambda s=s, var=var, bk=bk: S.copy(out=ksT[:, s, var, :], in_=ps[:, bk, 0:128]),
                          r=[bank[bk]], w=[r_samp2])
                k = cnt["gu"]
                cnt["gu"] += 1
                bk = k % 4
                tok0 = lis * 128 + 32 * s
                for p in range(2):
                    for kc in range(KC):
                        P.pe(lambda kc=kc, p=p, bk=bk, tok0=tok0: T.matmul(
                            ps[64 * p:64 * p + 32, bk, 0:128], lhsT=hT[:, kc, tok0:tok0 + 32], rhs=Wv[:, kc, :],
                            start=(kc == 0), stop=(kc == KC - 1)), r=[r_Wv] + rh(tok0, 32), w=[bank[bk]], sig=(kc == KC - 1))
                for p in range(2):
                    P.act(lambda p=p, bk=bk, s=s: S.copy(
                        out=Vs[64 * p:64 * p + 32, s, 2, :, 64 * p:64 * p + 64],
                        in_=ps[64 * p:64 * p + 32, bk, 0:128].rearrange("q (h d) -> q h d", h=2)),
                        r=[bank[bk]], w=[r_samp2])
            for s in range(2):
                tok0 = lis * 128 + 32 * s
                for h in range(2):
                    kbs = []
                    for i in range(2):
                        kbs.append((lambda p, s=s, h=h, i=i: ksT[:, s, 2 * h + p, 64 * i:64 * i + 64], 64, C_ZERO,
                                    Vs[:, s, i, h, :], r_samp2, r_samp2))
                    kbs.append((lambda p, h=h, tok0=tok0: kTp[:, 2 * h + p, 128 + tok0:128 + tok0 + 32], 32, C_ZERO,
                                Vs[:, s, 2, h, :], r_kTp, r_samp2))
                    attn_block(kbs, lambda h=h, tok0=tok0: [(qT[:, tok0 // 64, 4 * h + g2, tok0 % 64:tok0 % 64 + 32], 32) for g2 in range(4)], 32,
                               lambda h=h, tok0=tok0: ybT[:, 4 * h:4 * h + 4, tok0:tok0 + 32], h, [1, 1, 3])
            attn_flush()
            P.dve(lambda: V.memset(ybT[:, :, lis * 128 + 64:lis * 128 + 128], 0.0), w=[r_ybT])
        chk('sattn')
        for li, t in enumerate(tiles):
            if t in (LAST_MAIN, SAMP_TILE):
                P.pe(lambda t=t: T.transpose(out=ps[:, 7, 0:128], in_=kfull_t[t][:], identity=id32[:]),
                     r=[r_kf[t], r_c], w=[bank[7]])
                P.act(lambda: S.copy(out=osb2[:, 0:128], in_=ps[:, 7, 0:128]), r=[bank[7]], w=[r_osb2])
                if t == LAST_MAIN:
                    out_dmas.append(P.dma("sp", lambda: nc.sync.dma_start(out=o_k_p, in_=osb2[:, 0:128]), r=[r_osb2]))
                else:
                    for s in range(2):
                        out_dmas.append(P.dma("sp", lambda s=s: nc.sync.dma_start(
                            out=o_k_s[s, 96:128, :], in_=osb2[32 * s:32 * s + 32, 0:128]), r=[r_osb2]))

        chk('kout')
        c_lo3 = first_main_li * 128
        grps3 = groups_of(c_lo3, CT)
        gate_blocks = [get_block() for _ in range(0)]
        for half in range(2):
            buf, rb, extra = get_block()
            for c4 in range(4):
                c = 4 * half + c4
                for gi, (g0, n) in enumerate(grps3):
                    k = cnt["gu"]
                    cnt["gu"] += 1
                    bg, bp = 2 * (k % 2), 2 * (k % 2) + 1
                    for kc in range(KC):
                        P.pe(lambda: T.matmul(ps[:, bg, 0:n], lhsT=buf[:, kc, 128 * c4:128 * c4 + 128],
                                              rhs=hT[:, kc, g0:g0 + n], start=(kc == 0), stop=(kc == KC - 1)),
                             r=rb + rh(g0, n), w=[bank[bg]], sig=(kc == KC - 1))
                    for g in range(4):
                        P.pe(lambda: T.matmul(ps[:, bp, 0:n], lhsT=Wpp[:, g, 128 * c:128 * c + 128],
                                              rhs=yaT[:, g, g0:g0 + n], start=(g == 0), stop=(g == 3)),
                             r=[r_Wpp, r_yaT], w=[bank[bp]], sig=(g == 3))
                    gt_, rgt = gtmp[k % 2], r_gtmp[k % 2]
                    P.act(lambda: S.activation(out=gt_[:, 0:n], in_=ps[:, bg, 0:n], func=AF.Sigmoid,
                                               bias=cv_[:, C_BGA + c:C_BGA + c + 1]), r=[bank[bg], r_c], w=[rgt])
                    P.dve(lambda: V.tensor_tensor(out=tT[:, c4, g0:g0 + n], in0=ps[:, bp, 0:n], in1=gt_[:, 0:n],
                                                  op=ALU.mult), r=[bank[bp], rgt], w=[r_uT])
            buf, rb, extra = get_block()
            for c4 in range(4):
                c = 4 * half + c4
                for gi, (g0, n) in enumerate(grps3):
                    k = cnt["gu"]
                    cnt["gu"] += 1
                    bg, bp = 2 * (k % 2), 2 * (k % 2) + 1
                    for kc in range(KC):
                        P.pe(lambda: T.matmul(ps[:, bg, 0:n], lhsT=buf[:, kc, 128 * c4:128 * c4 + 128],
                                              rhs=hT[:, kc, g0:g0 + n], start=(kc == 0), stop=(kc == KC - 1)),
                             r=rb + rh(g0, n), w=[bank[bg]], sig=(kc == KC - 1))
                    for m in range(8):
                        P.pe(lambda: T.matmul(ps[:, bp, 0:n], lhsT=Wap[:, m, 128 * c:128 * c + 128],
                                              rhs=ybT[:, m, g0:g0 + n], start=(m == 0), stop=(m == 7)),
                             r=[r_Wap, r_ybT], w=[bank[bp]], sig=(m == 7))
                    gt_, rgt = gtmp[k % 2], r_gtmp[k % 2]
                    P.act(lambda: S.activation(out=gt_[:, 0:n], in_=ps[:, bg, 0:n], func=AF.Sigmoid,
                                               bias=cv_[:, C_BGB + c:C_BGB + c + 1]), r=[bank[bg], r_c], w=[rgt])
                    P.dve(lambda: V.tensor_tensor(out=gt_[:, 0:n], in0=ps[:, bp, 0:n], in1=gt_[:, 0:n], op=ALU.mult),
                          r=[bank[bp], rgt], w=[rgt])
                    P.dve(lambda: V.tensor_tensor(out=mT[:, c, g0:g0 + n], in0=tT[:, c4, g0:g0 + n], in1=gt_[:, 0:n],
                                                  op=ALU.add), r=[r_uT, rgt], w=[r_qT])
        if pre_out is not None:
            pre_out()
        for li in range(first_main_li, nt_sg):
            k = cnt["wo"]
            cnt["wo"] += 1
            b0 = 4 + 2 * (k % 2)
            for nh in range(2):
                for c in range(8):
                    P.pe(lambda c=c, nh=nh, li=li, b0=b0: T.matmul(ps[:, b0 + nh, :], lhsT=mT[:, c, li * 128:li * 128 + 128],
                                                                   rhs=Wom[:, c, nh * 512:nh * 512 + 512], start=(c == 0),
                                                                   stop=(c == 7)), r=[r_qT, r_Wom], w=[bank[b0 + nh]], sig=(c == 7))
            P.dve(lambda li=li, b0=b0: V.tensor_tensor(out=X[:, li, :], in0=ps[:, b0:b0 + 2, :].rearrange("p a b -> p (a b)"),
                                                       in1=X[:, li, :], op=ALU.add),
                  r=[bank[b0], bank[b0 + 1], rX[li]], w=[rX[li]])
            if hook is not None:
                hook(li)
        if sgi + 1 < len(SGS):
            P.act(lambda: S.copy(out=kT_hist[:], in_=kTp[:, :, CT:CT + 128]), r=[r_kTp], w=[r_hist])
            P.act(lambda: S.copy(out=V_hist[:], in_=Vbd[:, 2 * nt_sg:2 * nt_sg + 2, :, :]), r=[r_Vbd], w=[r_hist])
            P.act(lambda: S.copy(out=u_hist[:], in_=uT[:, :, CT:CT + 16]), r=[r_uT], w=[r_hist])

    kfull_t = {LAST_MAIN: kfull, SAMP_TILE: kfull2}
    r_kf = {LAST_MAIN: r_kfull, SAMP_TILE: R("kfull2")}

    try:
        for sgi, tiles in enumerate(SGS):
            tl = [(li, t) for li, t in enumerate(tiles)]
            tl2 = [(li, t) for li, t in enumerate(tiles) if t != 0]
            if sgi == 0:
                load_g(0)
                for li, t in enumerate(tiles):
                    load_x(sgi, li)
                norm_stage([(li, li * 128) for li, t in enumerate(tiles)])
            chk('norm1')
            ffn(sgi, 1, tl, store=False, pre_loop=lambda: (load_g(1), norm_begin(len(tl))), hook=lambda li: norm_push(li, li * 128, 0))
            chk('ffn1')
            mix(sgi, tiles, pre_out=lambda: (load_g(2), norm_begin(len(tl2))), hook=lambda li: norm_push(li, li * 128, 0))
            chk('mix')
            if sgi + 1 < len(SGS):
                nxt = SGS[sgi + 1]

                def pre2(sgi=sgi, nxt=nxt, tl2=tl2):
                    load_g(0)
                    norm_begin(len(nxt))
                    done_li = set(li for (li, _) in tl2)
                    for li in range(len(nxt)):
                        if li not in done_li:
                            norm_push(li, li * 128, 0)

                def hook2(li, nxt=nxt):
                    if li < len(nxt):
                        norm_push(li, li * 128, 0)
                ffn(sgi, 2, tl2, store=True, pre_loop=pre2, hook=hook2, guard=True, hook_lag=2)
            else:
                ffn(sgi, 2, tl2, store=True, guard=True)
    except _Stop:
        pass
    P.add("sp", lambda: nc.sync.nop(), extra=out_dmas, sig=False)
    P.emit()
    return nc


_NC_CACHE = {}


def make_core_inputs(inputs, nt_main, n_cores=8):
    f = lambda a: np.ascontiguousarray(np.asarray(a, dtype=np.float32))
    xp, xs = f(inputs["x_prompt"]), f(inputs["x_sample"])
    T_MAIN = nt_main * 128
    NT = nt_main + 2
    ident = np.eye(128, dtype=np.float32)
    onesbd = np.zeros((128, 128), np.float32)
    onesbd[:64, :64] = 1.0
    onesbd[64:, 64:] = 1.0
    onesbd32 = np.zeros((128, 128), np.float32)
    onesbd32[0:32, 0:64] = 1.0
    onesbd32[64:96, 64:128] = 1.0
    cst = np.concatenate([ident, onesbd, onesbd / 64.0, onesbd32], axis=1)
    g3 = np.stack([f(inputs["norm_ffn1"])[0], f(inputs["norm_mix"])[0], f(inputs["norm_ffn2"])[0]])
    g3 = np.ascontiguousarray(np.broadcast_to(g3[:, None, :], (3, 128, D)))
    qn, kn = f(inputs["q_norm"])[0], f(inputs["k_norm"])[0]
    bg = f(inputs["b_gate"])[0]
    psc, pb = f(inputs["pool_scale"])[0], f(inputs["pool_b"])[0]
    sinks = f(inputs["sinks"])[0]
    shared = dict(
        f1wi=f(inputs["ffn1_w_in"])[0], f1wo=f(inputs["ffn1_w_out"])[0], wmix=f(inputs["w_in"])[0],
        poolw=f(inputs["pool_w"])[0], wpp=f(inputs["w_pool_proj"])[0], wap=f(inputs["w_attn_proj"])[0],
        wom=f(inputs["w_out"])[0], f2wi=f(inputs["ffn2_w_in"])[0], f2wo=f(inputs["ffn2_w_out"])[0],
        g3=g3, cst=cst)
    maps = []
    pidx = np.arange(128)
    for c in range(n_cores):
        b, half = c // 2, c % 2
        xin = np.zeros((NT * 128, D), np.float32)
        if half == 1:
            xin[0:128] = xp[b, T_MAIN - 128:T_MAIN]
        xin[128:128 + T_MAIN] = xp[b, half * T_MAIN:(half + 1) * T_MAIN]
        xin[128 + T_MAIN:128 + T_MAIN + 32] = xs[2 * c]
        xin[128 + T_MAIN + 32:128 + T_MAIN + 64] = xs[2 * c + 1]
        colv = np.zeros((128, NCOL), np.float32)
        colv[:, C_GQ] = qn[pidx % 64]
        colv[:, C_GK] = kn[pidx % 64]
        for cc in range(8):
            colv[:, C_BGA + cc] = bg[0, cc * 128:(cc + 1) * 128]
            colv[:, C_BGB + cc] = bg[1, cc * 128:(cc + 1) * 128]
        for g in range(4):
            colv[:, C_PSC + g] = psc[g * 128:(g + 1) * 128]
            colv[:, C_PB + g] = pb[g]
        for h in range(2):
            for g2 in range(4):
                colv[0:64, C_SINK + 4 * h + g2] = sinks[h * 8 + 2 * g2]
                colv[64:128, C_SINK + 4 * h + g2] = sinks[h * 8 + 2 * g2 + 1]
        colv[:, C_MASK] = -30000.0 if half == 0 else 0.0
        colv[:, C_EPS] = EPS
        for g in range(4):
            w_ = 2 ** (g + 1)
            for j in range(16):
                colv[:, C_INV + 16 * g + j] = (float(w_) / min(j + 1, w_)) if half == 0 else 1.0
        m = dict(shared)
        m.update(xin=xin, colv=colv,
                 spool=f(inputs["state_pool"])[0, 2 * c:2 * c + 2],
                 ck=f(inputs["cache_k"])[0, 2 * c:2 * c + 2].reshape(2, 128, 128),
                 cv=f(inputs["cache_v"])[0, 2 * c:2 * c + 2].reshape(2, 128, 128))
        maps.append(m)
    return maps


def assemble(results, nt_main, n_cores=8):
    T_MAIN = nt_main * 128
    B = n_cores // 2
    S = 2 * T_MAIN
    yp = np.zeros((B, S, D), np.float32)
    ys = np.zeros((2 * n_cores, 32, D), np.float32)
    pp = np.zeros((1, B, 15, 512), np.float32)
    kp = np.zeros((1, B, 128, 2, 64), np.float32)
    vp = np.zeros((1, B, 128, 2, 64), np.float32)
    pss = np.zeros((1, 2 * n_cores, 15, 512), np.float32)
    ks = np.zeros((1, 2 * n_cores, 128, 2, 64), np.float32)
    vs = np.zeros((1, 2 * n_cores, 128, 2, 64), np.float32)
    for c, r in enumerate(results):
        b, half = c // 2, c % 2
        yp[b, half * T_MAIN:(half + 1) * T_MAIN] = r["y_main"]
        ys[2 * c] = r["y_samp"][0:32]
        ys[2 * c + 1] = r["y_samp"][32:64]
        if half == 1:
            pp[0, b] = r["o_pool_p"]
            kp[0, b] = r["o_k_p"].reshape(128, 2, 64)
            vp[0, b] = r["o_v_p"].reshape(128, 2, 64)
        pss[0, 2 * c:2 * c + 2] = r["o_pool_s"]
        ks[0, 2 * c:2 * c + 2] = r["o_k_s"].reshape(2, 128, 2, 64)
        vs[0, 2 * c:2 * c + 2] = r["o_v_s"].reshape(2, 128, 2, 64)
    return (yp, ys, pp, kp, vp, pss, ks, vs)


def kernel(**inputs):
    nt_main = 16
    if nt_main not in _NC_CACHE:
        _NC_CACHE[nt_main] = build(nt_main)
    nc = _NC_CACHE[nt_main]
    maps = make_core_inputs(inputs, nt_main)
    res = run_bass_kernel_spmd(nc, maps, core_ids=list(range(8)))
    return assemble(res.results, nt_main)
```
